# Optimizing a Trainium2 kernel written in Bass

```python
import jax, jax.numpy as jnp
from jax import lax
import numpy as np

D_MODEL = 2048
BATCH = 4
SEQ = 2048
DEPTH = 1
DEC_BATCH = 128
DEC_SEQ = 4
PAST_LEN = 16384
PAGE_SIZE = 128

CONV_DIM = D_MODEL // 2
CONV_GROUPS = 8
CONV_WIDTH = 3
SG_DIM = D_MODEL // 2
SG_GROUPS = 8
SG_HEAD = SG_DIM // SG_GROUPS
CHUNK = 128
D_FF = -(-8 * D_MODEL // (3 * 256)) * 256
N_MOD = 6
EPS = 1e-6
IN_SPLITS = (CONV_DIM, CONV_DIM, CONV_DIM, SG_DIM, SG_DIM, D_MODEL, D_MODEL)
IN_COLS = sum(IN_SPLITS)

kernel_name = "hybrid_shortconv_gmlp_decoder_step"


def _rms(x, gain):
    xf = x.astype(jnp.float32)
    y = xf * lax.rsqrt(jnp.mean(xf * xf, axis=-1, keepdims=True) + EPS)
    return (y * gain.astype(jnp.float32)).astype(x.dtype)


def _layer(x, c, conv_prefix, chunk, g_mix, g_ffn, w_ada, b_ada, w_in, w_conv, g_v, w_sg, b_sg,
           w_pa, w_pb, w_out, w_ffn_in, w_ffn_out):
    n, s, _ = x.shape
    mod = (jax.nn.silu(c) @ w_ada + b_ada).reshape(n, N_MOD, 1, D_MODEL)
    sh_m, sc_m, gt_m, sh_f, sc_f, gt_f = [mod[:, i] for i in range(N_MOD)]

    h = _rms(x, g_mix) * (1 + sc_m) + sh_m
    proj = h @ w_in
    idx = np.cumsum(IN_SPLITS)[:-1].tolist()
    b_gate, c_gate, hc, u, v, ga, gb = jnp.split(proj, idx, axis=-1)

    z = c_gate * hc
    zfull = jnp.concatenate([conv_prefix.astype(z.dtype), z], axis=1)
    conv = sum(w_conv[k] * zfull[:, k:k + s] for k in range(CONV_WIDTH))
    ya = b_gate * conv

    u = jax.nn.gelu(u)
    v = _rms(jax.nn.gelu(v), g_v)
    vg = v.reshape(n, s // chunk, chunk, SG_GROUPS, SG_HEAD)
    w_causal = jnp.tril(w_sg[:, :chunk, :chunk])
    sp = jnp.einsum('gts,bnsgc->bntgc', w_causal, vg) + b_sg[:, :chunk].T[None, None, :, :, None]
    yb = u * sp.reshape(n, s, SG_DIM)

    merged = jax.nn.sigmoid(ga) * (ya @ w_pa) + jax.nn.sigmoid(gb) * (yb @ w_pb)
    x = x + gt_m * (merged @ w_out)

    h2 = _rms(x, g_ffn) * (1 + sc_f) + sh_f
    gate, up = jnp.split(h2 @ w_ffn_in, 2, axis=-1)
    x = x + gt_f * ((jax.nn.silu(gate) * up) @ w_ffn_out)

    conv_state = zfull[:, -(CONV_WIDTH - 1):]
    v_rows = vg[:, -1]
    return x, conv_state, v_rows


def setup_inputs(seed: int = 0) -> dict:
    key = jax.random.key(seed)
    ks = jax.random.split(key, 24)
    nrm = lambda k, shape, scale: jax.random.normal(k, shape, jnp.float32) * scale
    L = DEPTH
    return {
        "x_prompt": nrm(ks[0], (BATCH, SEQ, D_MODEL), 1.0),
        "x_sample": nrm(ks[1], (DEC_BATCH, DEC_SEQ, D_MODEL), 1.0),
        "state_conv": nrm(ks[2], (L, DEC_BATCH, CONV_WIDTH - 1, CONV_DIM), 1.0),
        "c_prompt": nrm(ks[3], (BATCH, D_MODEL), 1.0),
        "c_sample": nrm(ks[4], (DEC_BATCH, D_MODEL), 1.0),
        "g_mix": 1.0 + nrm(ks[5], (L, D_MODEL), 0.02),
        "g_ffn": 1.0 + nrm(ks[6], (L, D_MODEL), 0.02),
        "w_ada": nrm(ks[7], (L, D_MODEL, N_MOD * D_MODEL), 0.5 * D_MODEL ** -0.5),
        "b_ada": nrm(ks[8], (L, N_MOD * D_MODEL), 0.02),
        "w_in": nrm(ks[9], (L, D_MODEL, IN_COLS), D_MODEL ** -0.5),
        "w_conv": nrm(ks[10], (L, CONV_WIDTH, CONV_DIM), CONV_WIDTH ** -0.5),
        "g_v": 1.0 + nrm(ks[11], (L, SG_DIM), 0.02),
        "w_sg": nrm(ks[12], (L, SG_GROUPS, CHUNK, CHUNK), CHUNK ** -0.5),
        "b_sg": 1.0 + nrm(ks[13], (L, SG_GROUPS, CHUNK), 0.02),
        "w_pa": nrm(ks[14], (L, CONV_DIM, D_MODEL), CONV_DIM ** -0.5),
        "w_pb": nrm(ks[15], (L, SG_DIM, D_MODEL), SG_DIM ** -0.5),
        "w_out": nrm(ks[16], (L, D_MODEL, D_MODEL), D_MODEL ** -0.5),
        "w_ffn_in": nrm(ks[17], (L, D_MODEL, 2 * D_FF), D_MODEL ** -0.5),
        "w_ffn_out": nrm(ks[18], (L, D_FF, D_MODEL), D_FF ** -0.5),
        "g_final": 1.0 + nrm(ks[19], (D_MODEL,), 0.02),
    }


def reference(x_prompt, x_sample, state_conv, c_prompt, c_sample, g_mix, g_ffn, w_ada, b_ada,
              w_in, w_conv, g_v, w_sg, b_sg, w_pa, w_pb, w_out, w_ffn_in, w_ffn_out, g_final):
    xp, xs = x_prompt, x_sample
    conv_p, conv_s, sgv_p, sgv_s = [], [], [], []
    for l in range(DEPTH):
        params = (g_mix[l], g_ffn[l], w_ada[l], b_ada[l], w_in[l], w_conv[l], g_v[l], w_sg[l],
                  b_sg[l], w_pa[l], w_pb[l], w_out[l], w_ffn_in[l], w_ffn_out[l])
        zero_prefix = jnp.zeros((xp.shape[0], CONV_WIDTH - 1, CONV_DIM), xp.dtype)
        xp, cp, vp = _layer(xp, c_prompt, zero_prefix, CHUNK, *params)
        xs, cs, vs = _layer(xs, c_sample, state_conv[l], xs.shape[1], *params)
        conv_p.append(cp); conv_s.append(cs); sgv_p.append(vp); sgv_s.append(vs)
    y_prompt = _rms(xp, g_final)
    y_sample = _rms(xs, g_final)
    return (y_prompt, y_sample, jnp.stack(conv_p), jnp.stack(conv_s), jnp.stack(sgv_p), jnp.stack(sgv_s))
```

```python
import contextlib
import os
import numpy as np
import concourse.bass as bass
import concourse.mybir as mybir
from concourse.bass_utils import run_bass_kernel_spmd

F32 = mybir.dt.float32
BF16 = mybir.dt.bfloat16
AF = mybir.ActivationFunctionType
ALU = mybir.AluOpType

NCORES = 8
D = 2048
KC = 16
TP = 1024
TS = 64
T = TP + TS
TH = T + 2
NB = 16
NR = 18
CD = 1024
DFF = 5632
NFF = DFF // 128
EPS = 1e-6
IN_COLS = 9216
TT = [(0, 363), (363, 363), (726, 362)]
TTSP = [(0, 512), (512, 512), (1024, 64)]
ENGS = ["pe", "act", "dve", "pool", "sp"]
STAGE = int(os.environ.get("KSTAGE", "9"))
DEBUG = int(os.environ.get("KDEBUG", "0"))


class Op:
    __slots__ = ("eng", "fn", "deps", "is_dma", "dsem", "dcount", "needs_inc", "event", "tag")

    def __init__(self, eng, fn, is_dma, dsem, tag):
        self.eng = eng
        self.fn = fn
        self.deps = []
        self.is_dma = is_dma
        self.dsem = dsem
        self.dcount = 0
        self.needs_inc = False
        self.event = 0
        self.tag = tag


def ap_ranges(ap):
    sp = str(ap.space)
    if sp == "DRAM":
        return []
    pst = ap.ap[0][0]
    esz = mybir.dt.size(ap.dtype)
    off = ap.offset % pst if pst > 0 else ap.offset
    dims = list(ap.ap[1:])
    name = ap.tensor.name

    def ext_of(ds):
        e = 1
        for (st, cnt) in ds:
            e += (cnt - 1) * abs(st)
        return e

    outs = []
    if len(dims) >= 2 and 1 < dims[0][1] <= 64 and abs(dims[0][0]) > ext_of(dims[1:]):
        inner = ext_of(dims[1:])
        for i in range(dims[0][1]):
            lo = off + i * dims[0][0]
            outs.append((lo * esz, (lo + inner) * esz))
    else:
        outs.append((off * esz, (off + ext_of(dims)) * esz))
    res = []
    for (lo, hi) in outs:
        if sp == "PSUM":
            lo = lo // 2048 * 2048
            hi = -(-hi // 2048) * 2048
        res.append((name, lo, hi))
    return res


class Sched:
    def __init__(self):
        self.ops = {e: [] for e in ENGS}
        self.ent = {}
        self.dma_counts = {}

    def add(self, eng, fn, reads=(), writes=(), dma=None, tag=""):
        op = Op(eng, fn, dma is not None, dma, tag)
        deps = []

        def need(o, kind):
            if o is None or o is op:
                return
            if o.eng != eng or o.is_dma or op.is_dma:
                deps.append(o)
            elif eng != "pe":
                deps.append(o)

        rd_ranges = [r for ap in reads for r in ap_ranges(ap)]
        wr_ranges = [r for ap in writes for r in ap_ranges(ap)]
        for (name, lo, hi) in rd_ranges:
            for e in self.ent.get(name, ()):
                if e[0] < hi and lo < e[1]:
                    need(e[2], "raw")
                    e[3].append(op)
        for (name, lo, hi) in wr_ranges:
            lst = self.ent.setdefault(name, [])
            new = []
            for e in lst:
                if e[0] < hi and lo < e[1]:
                    need(e[2], "waw")
                    for rd in e[3]:
                        need(rd, "war")
                    if e[0] < lo:
                        new.append([e[0], lo, e[2], list(e[3])])
                    if e[1] > hi:
                        new.append([hi, e[1], e[2], list(e[3])])
                else:
                    new.append(e)
            new.append([lo, hi, op, []])
            self.ent[name] = new
        seen = set()
        for d in deps:
            if id(d) not in seen:
                seen.add(id(d))
                op.deps.append(d)
                if not d.is_dma:
                    d.needs_inc = True
        if op.is_dma:
            self.dma_counts[dma] = self.dma_counts.get(dma, 0) + 1
            op.dcount = self.dma_counts[dma]
        self.ops[eng].append(op)
        return op

    def emit(self, nc):
        for e in ENGS:
            c = 0
            for op in self.ops[e]:
                if op.needs_inc:
                    c += 1
                    op.event = c
        with contextlib.ExitStack() as st:
            esem = {e: st.enter_context(nc.semaphore("e_" + e)) for e in ENGS}
            dsem = {n: st.enter_context(nc.semaphore("d_" + n)) for n in self.dma_counts}
            block = st.enter_context(nc.Block())
            sched = self

            def body(ename):
                def _f(eng):
                    waited = {}
                    for op in sched.ops[ename]:
                        for d in op.deps:
                            if d.is_dma:
                                key, val, sem = ("d", d.dsem), 16 * d.dcount, dsem[d.dsem]
                            else:
                                key, val, sem = ("e", d.eng), d.event, esem[d.eng]
                            if waited.get(key, 0) >= val:
                                continue
                            waited[key] = val
                            eng.wait_ge(sem, val)
                        ins = op.fn(eng)
                        if op.is_dma:
                            ins.then_inc(dsem[op.dsem], 16)
                        elif op.needs_inc:
                            ins.then_inc(esem[ename], 1)
                    if ename == "sp":
                        for n, c in sched.dma_counts.items():
                            eng.wait_ge(dsem[n], 16 * c)
                return _f

            block.tensor(body("pe"))
            block.scalar(body("act"))
            block.vector(body("dve"))
            block.gpsimd(body("pool"))
            block.sync(body("sp"))

    def act(self, out, in_, func, bias=None, scale=None, accum_out=None, tag=""):
        kw = {}
        rd = [in_]
        wr = [out]
        if bias is not None:
            kw["bias"] = bias
            if not isinstance(bias, (int, float)):
                rd.append(bias)
        if scale is not None:
            kw["scale"] = scale
            if not isinstance(scale, (int, float)):
                rd.append(scale)
        if accum_out is not None:
            kw["accum_out"] = accum_out
            wr.append(accum_out)
        return self.add("act", lambda e: e.activation(out=out, in_=in_, func=func, **kw), rd, wr, tag=tag)

    def tt(self, out, in0, in1, op, tag=""):
        return self.add("dve", lambda e: e.tensor_tensor(out=out, in0=in0, in1=in1, op=op), [in0, in1], [out], tag=tag)

    def stt(self, out, in0, scalar, in1, op0, op1, tag=""):
        rd = [in0, in1] + ([] if isinstance(scalar, (int, float)) else [scalar])
        return self.add("dve", lambda e: e.scalar_tensor_tensor(out=out, in0=in0, scalar=scalar, in1=in1, op0=op0, op1=op1),
                        rd, [out], tag=tag)

    def ts(self, out, in0, s1, s2, op0, op1=None, tag=""):
        rd = [in0] + [s for s in (s1, s2) if s is not None and not isinstance(s, (int, float))]
        if op1 is None:
            return self.add("dve", lambda e: e.tensor_scalar(out=out, in0=in0, scalar1=s1, scalar2=None, op0=op0), rd, [out], tag=tag)
        return self.add("dve", lambda e: e.tensor_scalar(out=out, in0=in0, scalar1=s1, scalar2=s2, op0=op0, op1=op1), rd, [out], tag=tag)

    def copy(self, eng, out, in_, tag=""):
        if eng == "act":
            return self.add("act", lambda e: e.copy(out=out, in_=in_), [in_], [out], tag=tag)
        return self.add(eng, lambda e: e.tensor_copy(out=out, in_=in_), [in_], [out], tag=tag)

    def recip(self, out, in_):
        return self.add("dve", lambda e: e.reciprocal(out=out, in_=in_), [in_], [out])

    def mms(self, lst, tag=""):
        rd, wr = [], []
        for (o, l, r, s0, s1) in lst:
            rd += [l, r]
            wr.append(o)

        def fn(e):
            ins = None
            for (o, l, r, s0, s1) in lst:
                ins = e.matmul(o, lhsT=l, rhs=r, start=s0, stop=s1)
            return ins
        return self.add("pe", fn, rd, wr, tag=tag)

    def transposes(self, lst, ident, tag=""):
        rd, wr = [], []
        for (o, i) in lst:
            rd.append(i)
            wr.append(o)

        def fn(e):
            ins = None
            for (o, i) in lst:
                k = i.ap[0][1]
                ins = e.transpose(out=o, in_=i, identity=ident[0:k, 0:k])
            return ins
        return self.add("pe", fn, rd + [ident[:]], wr, tag=tag)

    def dma(self, eng, out, in_, sem, slow=False, tag=""):
        if slow:
            f = lambda e: e.dma_start(out=out, in_=in_, allow_slow_non_contiguous=True)
        else:
            f = lambda e: e.dma_start(out=out, in_=in_)
        return self.add(eng, f, [in_], [out], dma=sem, tag=tag)


class PanelStream:
    NSLOT = 4

    def __init__(self, S, ring, plan=None):
        self.S = S
        self.ring = ring
        self.plan = plan
        self.rec = []
        self.next_load = 0
        self.next_use = 0
        self.loaded_slots = 0
        self.freed_slots = 0
        self.use_slot = 0
        self.pending_free = []
    def start(self):
        if self.plan is not None:
            self._fill()

    @staticmethod
    def _width(nk, ncols):
        return 1 if nk * ncols <= 4096 else 2

    def _view(self, slot0, nk, ncols):
        wd = self._width(nk, ncols)
        slot = slot0 % self.NSLOT
        if wd == 1:
            return self.ring[:, slot, 0:nk * ncols].rearrange("p (k n) -> p k n", n=ncols)
        assert slot % 2 == 0
        return self.ring[:, slot:slot + 2, :].rearrange("p s n -> p (s n)")[:, 0:nk * ncols].rearrange("p (k n) -> p k n", n=ncols)

    def _fill(self):
        while self.next_load < len(self.plan):
            (w, r0, nk, c0, ncols) = self.plan[self.next_load]
            wd = self._width(nk, ncols)
            if self.loaded_slots - self.freed_slots + wd > self.NSLOT:
                return
            src = w[r0:r0 + nk * 128, c0:c0 + ncols].rearrange("(k p) n -> p k n", p=128)
            dst = self._view(self.loaded_slots, nk, ncols)
            self.S.dma("pool", dst, src, "ring%d" % (self.loaded_slots % self.NSLOT), tag="panel%d" % self.next_load)
            self.loaded_slots += wd
            self.next_load += 1

    def acquire(self, w, r0, nk, c0, ncols):
        i = self.next_use
        self.next_use += 1
        if self.plan is None:
            self.rec.append((w, r0, nk, c0, ncols))
        else:
            assert self.plan[i][1:] == (r0, nk, c0, ncols), (i, self.plan[i][1:], (r0, nk, c0, ncols))
        v = self._view(self.use_slot, nk, ncols)
        wd = self._width(nk, ncols)
        self.use_slot += wd
        self.pending_free.append(wd)
        return v

    def release(self):
        wd = self.pending_free.pop(0)
        self.freed_slots += wd
        if self.plan is not None:
            self._fill()


def V(arena, boff, dtype, dims):
    esz = mybir.dt.size(dtype)
    n = 1
    for d in dims:
        n *= d
    assert boff % 4 == 0 and (n * esz) % 4 == 0
    ap = arena[:, boff // 4:(boff + n * esz) // 4]
    if dtype != F32:
        ap = ap.bitcast(dtype)
    if len(dims) == 2:
        ap = ap.rearrange("p (a b) -> p a b", b=dims[1])
    elif len(dims) == 3:
        ap = ap.rearrange("p (a b c) -> p a b c", b=dims[1], c=dims[2])
    return ap


def build_nc():
    nc = bass.Bass("TRN2", target_bir_lowering=False)

    def din(name, shape):
        return nc.dram_tensor(name, list(shape), F32, kind="ExternalInput").ap()

    def dout(name, shape):
        return nc.dram_tensor(name, list(shape), F32, kind="ExternalOutput").ap()

    I = dict(
        xp=din("xp", [TP, D]), xsh=din("xsh", [TS + 2, D]), hmask=din("hmask", [1, 1]),
        sc=din("sc", [2 * NB, CD]), cvec=din("cvec", [NB + 1, D]),
        w_ada=din("w_ada", [D, 6 * D]), b_ada=din("b_ada", [96, 128]),
        w_in=din("w_in", [D, IN_COLS]), w_conv=din("w_conv", [3, CD]), g_v=din("g_v", [1, CD]),
        w_sg=din("w_sg", [8, 128, 128]), b_sg=din("b_sg", [1, 1024]),
        w_pa=din("w_pa", [CD, D]), w_pb=din("w_pb", [CD, D]), w_out=din("w_out", [D, D]),
        w_ffn_in=din("w_ffn_in", [D, 2 * DFF]), w_ffn_out=din("w_ffn_out", [DFF, D]),
        g_mix=din("g_mix", [1, D]), g_ffn=din("g_ffn", [1, D]), g_final=din("g_final", [1, D]),
    )
    O = dict(
        yp=dout("yp", [TP, D]), ys=dout("ys", [TS, D]), cvo=dout("cvo", [2 + 2 * NB, CD]),
        sgp=dout("sgp", [128, CD]), sgs=dout("sgs", [TS, CD]),
    )
    if DEBUG:
        O["dbg"] = dout("dbg", [128, 8192])

    with contextlib.ExitStack() as st:
        def sb(name, shape, dt=F32):
            return st.enter_context(nc.sbuf_tensor(name, list(shape), dt))

        R1 = sb("R1", [128, 17440])
        R2 = sb("R2", [128, 17440])
        MISC = sb("MISC", [128, 2176])
        ring = sb("ring", [128, 4, 4096], BF16)
        PS = st.enter_context(nc.psum_tensor("PS", [128, 8, 512], F32))
        C = dict(
            ident=sb("ident", [128, 128]), cmask=sb("cmask", [128, 128]), bmask=sb("bmask", [64, 64]), W4T=sb("W4T", [4, 8, 4]), E4=sb("E4", [4, 64]),
            ones1=sb("ones1", [1, 128]), onesbf=sb("onesbf", [128, 128], BF16),
            bcomb=sb("bcomb", [128, 8, 128], BF16), bcomb_s=sb("bcomb_s", [128, 8, 64], BF16), bsel=sb("bsel", [128, 128], BF16),
            gT=sb("gT", [128, 32]), gnat=sb("gnat", [32, 128]), wconvT=sb("wconvT", [128, 8, 3]),
            hmaskB=sb("hmaskB", [128, 1]), bT=sb("bT", [128, 96]), bnat=sb("bnat", [96, 128]),
            csilT=sb("csilT", [128, 16, NR], BF16), modtok=sb("modtok", [NR, 512]),
            modT=sb("modT", [128, 96, NR]), a1=sb("a1", [128, 16, NR]), a2=sb("a2", [128, 16, NR]),
            WT=sb("WT", [128, 8, 128], BF16), Wblk=sb("Wblk", [64, 8, 64], BF16), scT=sb("scT", [128, 8, 32]),
            ss=sb("ss", [128, 16]), sd=sb("sd", [128, 16]), rstd=sb("rstd", [128, 16]),
            vss=sb("vss", [128, 16]), vsd=sb("vsd", [128, 16]), vrs=sb("vrs", [128, 16]),
            fss=sb("fss", [128, 16]), fq=sb("fq", [128, 8]), tmp64=sb("tmp64", [128, TS]), fsd=sb("fsd", [128, 16]), frs=sb("frs", [128, 16]),
        )

        def construct(S, PSt):
            ident = C["ident"]
            modT = C["modT"]
            bank_ctr = [0]
            set_ctr = [0]

            def next_set():
                s = set_ctr[0] % 2
                set_ctr[0] += 1
                return (3 * s, 3 * s + 1, 3 * s + 2)

            S.add("pool", lambda e: e.memset(ident[:], 1.0), [], [ident[:]])
            S.add("pool", lambda e: e.affine_select(out=ident[:], in_=ident[:], pattern=[[-1, 128]], base=0,
                                                      channel_multiplier=1, compare_op=ALU.is_equal, fill=0.0),
                  [ident[:]], [ident[:]])
            cst = V(R2, 0, F32, [D])[0:NR]
            S.add("dve", lambda e: e.memset(cst, 0.0), [], [cst])
            for qq in range(4):
                S.dma("sp", cst[0:NB + 1, qq * 512:(qq + 1) * 512], I["cvec"][:, qq * 512:(qq + 1) * 512], "p_c%d" % qq)
            PSt.start()
            cmask = C["cmask"]
            S.add("pool", lambda e: e.memset(cmask[:], 1.0), [], [cmask[:]])
            S.add("pool", lambda e: e.affine_select(out=cmask[:], in_=cmask[:], pattern=[[1, 128]], base=0,
                                                      channel_multiplier=-1, compare_op=ALU.is_ge, fill=0.0),
                  [cmask[:]], [cmask[:]])
            ones1, onesbf = C["ones1"], C["onesbf"]
            S.add("pool", lambda e: e.memset(C["vss"][:], 1.0), [], [C["vss"][:]])
            S.add("pool", lambda e: e.memset(ones1[:], 1.0), [], [ones1[:]])
            bcomb, bcomb_s, bsel = C["bcomb"], C["bcomb_s"], C["bsel"]
            S.add("pool", lambda e: e.memset(bcomb[:], 0.0), [], [bcomb[:]])
            S.add("pool", lambda e: e.memset(bsel[:], 0.0), [], [bsel[:]])
            S.add("pool", lambda e: e.memset(bsel[0:1, :], 1.0), [bsel[:]], [bsel[0:1, :]])
            S.add("pool", lambda e: e.memset(bsel[32:33, :], 1.0), [bsel[:]], [bsel[32:33, :]])
            S.add("pool", lambda e: e.memset(onesbf[:], 1.0), [], [onesbf[:]])
            bmask = C["bmask"]
            E4 = C["E4"]
            S.add("pool", lambda e: e.memset(E4[:], 1.0), [], [E4[:]])
            S.add("pool", lambda e: e.affine_select(out=E4[:], in_=E4[:], pattern=[[0, 16], [-1, 4]], base=0,
                                                      channel_multiplier=1, compare_op=ALU.is_equal, fill=0.0),
                  [E4[:]], [E4[:]])
            S.add("pool", lambda e: e.memset(bmask[:], 1.0), [], [bmask[:]])
            S.add("pool", lambda e: e.affine_select(out=bmask[:], in_=bmask[:], pattern=[[1, 64]], base=0,
                                                      channel_multiplier=-1, compare_op=ALU.is_ge, fill=0.0),
                  [bmask[:]], [bmask[:]])
            S.add("pool", lambda e: e.affine_select(out=bmask[:], in_=bmask[:], pattern=[[-4, 16], [0, 4]], base=0,
                                                      channel_multiplier=1, compare_op=ALU.is_ge, fill=0.0),
                  [bmask[:]], [bmask[:]])

            S.dma("sp", C["bnat"][:], I["b_ada"][:, :], "p_b")
            gnat = C["gnat"]
            S.dma("sp", gnat[0:16, :], I["g_mix"].rearrange("o (k p) -> (o k) p", p=128), "p_g1")
            S.dma("sp", gnat[16:32, :], I["g_ffn"].rearrange("o (k p) -> (o k) p", p=128), "p_g2")
            sct = V(MISC, 4096, F32, [CD])[0:2 * NB]
            S.dma("sp", sct, I["sc"][:, :], "p_sc")

            S.act(cst, cst, AF.Silu)
            S.transposes([(PS[:, 6, k * NR:(k + 1) * NR], cst[:, k * 128:(k + 1) * 128]) for k in range(KC)], ident)
            S.copy("dve", C["csilT"][:], PS[:, 6, 0:KC * NR].rearrange("p (k r) -> p k r", r=NR))
            S.transposes([(PS[:, 7, 0:96], C["bnat"][:]), (PS[:, 7, 96:128], C["gnat"][:])], ident)
            S.copy("dve", C["bT"][:], PS[:, 7, 0:96])
            S.copy("dve", C["gT"][:], PS[:, 7, 96:128])

            ada_state = [0]
            pend = []

            def flush():
                for f in pend:
                    f()
                del pend[:]

            def ada_round():
                r = ada_state[0]
                if r >= 24:
                    return
                ada_state[0] += 1
                if r < 8:
                    pan = PSt.acquire(I["w_ada"], 0, KC, r * 512, 512)
                    S.mms([(PS[0:NR, 6, :], C["csilT"][:, k, :], pan[:, k, :], k == 0, k == KC - 1)
                           for k in range(KC)], tag="ada%d" % r)
                    PSt.release()
                else:
                    for half in range(2):
                        pan = PSt.acquire(I["w_ada"], 0, KC, r * 512 + half * 256, 256)
                        S.mms([(PS[0:NR, 6, half * 256:(half + 1) * 256], C["csilT"][:, k, :], pan[:, k, :], k == 0, k == KC - 1)
                               for k in range(KC)], tag="ada%d" % r)
                        PSt.release()
                mt = C["modtok"]
                S.copy("act", mt[:], PS[0:NR, 6, :])

                def follow(r=r, mt=mt):
                    S.transposes([(PS[:, 7, kk * NR:(kk + 1) * NR], mt[:, kk * 128:(kk + 1) * 128]) for kk in range(4)], ident)
                    S.tt(modT[:, 4 * r:4 * r + 4, :], PS[:, 7, 0:4 * NR].rearrange("p (k r) -> p k r", r=NR),
                         C["bT"][:, 4 * r:4 * r + 4].unsqueeze(2).broadcast_to([128, 4, NR]), ALU.add)
                    if r == 7:
                        S.ts(C["a1"][:], modT[:, 16:32, :], 1.0, None, ALU.add)
                        S.tt(C["a1"][:], C["a1"][:], C["gT"][:, 0:16].unsqueeze(2).broadcast_to([128, 16, NR]), ALU.mult)
                    if r == 19:
                        S.ts(C["a2"][:], modT[:, 64:80, :], 1.0, None, ALU.add)
                        S.tt(C["a2"][:], C["a2"][:], C["gT"][:, 16:32].unsqueeze(2).broadcast_to([128, 16, NR]), ALU.mult)
                pend.append(follow)

            def ada_all():
                while ada_state[0] < 24:
                    ada_round()
                    flush()

            xnT = V(R2, 0, F32, [KC, TH])
            hT = V(R1, 0, BF16, [KC, TH])
            xt = [V(R1, 34880 + i * 8192, F32, [D]) for i in range(2)]
            xn = [V(R1, 34880 + 16384 + i * 8192, F32, [D]) for i in range(2)]
            ss, sd, rstd = C["ss"], C["sd"], C["rstd"]
            def p0_stage1(c):
                rows = 128 if c < 8 else TS + 2
                src = I["xp"][c * 128:(c + 1) * 128, :] if c < 8 else I["xsh"][:, :]
                xt_c = xt[c % 2][0:rows]
                xn_c = xn[c % 2][0:rows]
                S.dma("sp", xt_c, src, "xt%d" % (c % 2))
                S.act(xn_c, xt_c, AF.Square, accum_out=ss[0:rows, c:c + 1])
                S.act(sd[0:rows, c:c + 1], ss[0:rows, c:c + 1], AF.Sqrt, bias=EPS, scale=1.0 / D)
                S.recip(rstd[0:rows, c:c + 1], sd[0:rows, c:c + 1])
                S.ts(xn_c, xt_c, rstd[0:rows, c:c + 1], None, ALU.mult)

            def p0_stage2(c):
                rows = 128 if c < 8 else TS + 2
                xn_c = xn[c % 2][0:rows]
                for q in range(4):
                    b = bank_ctr[0] % 6
                    bank_ctr[0] += 1
                    S.transposes([(PS[:, b, kk * 128:kk * 128 + rows], xn_c[:, (4 * q + kk) * 128:(4 * q + kk + 1) * 128])
                                  for kk in range(4)], ident)
                    src_ps = PS[:, b, :].rearrange("p (k t) -> p k t", t=128)[:, :, 0:rows]
                    dst = xnT[:, 4 * q:4 * q + 4, c * 128:c * 128 + rows]
                    S.copy("act" if q % 2 == 0 else "dve", dst, src_ps)
                flush()
                if c < 8:
                    ada_round()

            p0_stage1(0)
            for c in range(9):
                if c + 1 < 9:
                    p0_stage1(c + 1)
                p0_stage2(c)
            flush()
            wsg_nat = V(MISC, 0, F32, [8, 128])
            S.dma("sp", wsg_nat, I["w_sg"].rearrange("g t s -> t g s"), "p_wsg")
            wc = I["w_conv"]
            for kk in range(3):
                S.dma("sp", C["wconvT"][:, :, kk], bass.AP(wc.tensor, kk * CD, [[1, 128], [128, 8]]), "p_wc", slow=True)
            S.dma("sp", C["hmaskB"][:], bass.AP(I["hmask"].tensor, 0, [[0, 128], [1, 1]]), "p_hm")
            wsg = I["w_sg"]
            W4T = C["W4T"]
            for t4 in range(4):
                S.dma("sp", W4T[:, :, t4], bass.AP(wsg.tensor, t4 * 128, [[1, 4], [128 * 128, 8]]), "p_w4", slow=True)
            a1 = C["a1"]
            xs_all = xnT[:, :, TP:T].rearrange("p k (b s) -> p k b s", s=4)
            S.tt(xs_all, xs_all, a1[:, :, 1:17].unsqueeze(3).broadcast_to([128, KC, NB, 4]), ALU.mult)
            xh_all = xnT[:, :, T:TH]
            S.tt(xh_all, xh_all, a1[:, :, 0:1].broadcast_to([128, KC, 2]), ALU.mult)
            for k in range(KC):
                if k % 2 == 0:
                    S.act(hT[:, k, 0:TP], xnT[:, k, 0:TP], AF.Identity, bias=modT[:, k, 0:1], scale=a1[:, k, 0:1])
                else:
                    S.ts(hT[:, k, 0:TP], xnT[:, k, 0:TP], a1[:, k, 0:1], modT[:, k, 0:1], ALU.mult, ALU.add)
            S.tt(hT[:, :, TP:T].rearrange("p k (b s) -> p k b s", s=4), xs_all,
                 modT[:, 0:KC, 1:17].unsqueeze(3).broadcast_to([128, KC, NB, 4]), ALU.add)
            S.tt(hT[:, :, T:TH], xh_all, modT[:, 0:KC, 0:1].broadcast_to([128, KC, 2]), ALU.add)

            btmp = V(MISC, 0, F32, [1024])
            bhi = V(MISC, 4096, BF16, [1024])

            def sg_setup():
                for hh in range(2):
                    S.transposes([(PS[:, 6 + hh, gg * 128:(gg + 1) * 128], wsg_nat[:, 4 * hh + gg, :]) for gg in range(4)], ident)
                    S.tt(C["WT"][:, 4 * hh:4 * hh + 4, :], PS[:, 6 + hh, :].rearrange("p (g t) -> p g t", t=128),
                         cmask[:].unsqueeze(1).broadcast_to([128, 4, 128]), ALU.mult)
                S.mms([(PS[0:TS, 6, 0:32], C["E4"][:], W4T[:].rearrange("p g t -> p (g t)"), True, True)])
                S.tt(C["Wblk"][:].rearrange("p g (b t) -> p g b t", t=4),
                     PS[0:TS, 6, 0:32].rearrange("p (g t) -> p g t", t=4).unsqueeze(2).broadcast_to([TS, 8, NB, 4]),
                     bmask[:].rearrange("p (b t) -> p b t", t=4).unsqueeze(1).broadcast_to([TS, 8, NB, 4]), ALU.mult)
                S.dma("sp", btmp[0:1, :], I["b_sg"][:, :], "p_bs0")
                S.dma("sp", btmp[32:33, :], I["b_sg"][:, :], "p_bs1")
                bc2 = bcomb[:].rearrange("p g t -> p (g t)")
                S.copy("dve", bc2[0:1, :], btmp[0:1, :])
                S.copy("dve", bhi[32:33, :], btmp[32:33, :])
                S.tt(bc2[32:33, :], btmp[32:33, :], bhi[32:33, :], ALU.subtract)
                S.copy("dve", bcomb_s[:].rearrange("p g (b t) -> p g b t", t=4),
                       bcomb[:, :, 0:4].unsqueeze(2).broadcast_to([128, 8, NB, 4]))
            S.transposes([(PS[:, 6, j * 32:(j + 1) * 32], sct[:, j * 128:(j + 1) * 128]) for j in range(8)], ident)
            S.copy("dve", C["scT"][:], PS[:, 6, 0:256].rearrange("p (j r) -> p j r", r=32))
            gvB = V(R2, 65536, F32, [CD])
            S.dma("sp", gvB, bass.AP(I["g_v"].tensor, 0, [[0, 128], [1, CD]]), "p_gv")
            pend.append(sg_setup)

            if STAGE < 1:
                ada_all()
                if DEBUG:
                    dbgt = V(MISC, 0, F32, [2176])
                    S.dma("sp", O["dbg"][:, 0:96 * NR], modT[:].rearrange("p a b -> p (a b)"), "o_dbg")
                    S.copy("dve", dbgt[:, 0:1090], hT[:, 3, :])
                    S.dma("sp", O["dbg"][:, 4096:4096 + 1090], dbgt[:, 0:1090], "o_dbg3")
                return

            def mm_group(bset, pan, i, rhs_fn, nk, ncols=(363, 363, 362), tag="", split_k=False):
                if split_k:
                    for k in range(nk):
                        S.mms([(PS[:, bset[t], 0:ncols[t]], pan[:, k, i * 128:(i + 1) * 128], rhs_fn(k, t, ncols[t]),
                                k == 0, k == nk - 1) for t in range(2)], tag=tag)
                    S.mms([(PS[:, bset[2], 0:ncols[2]], pan[:, k, i * 128:(i + 1) * 128], rhs_fn(k, 2, ncols[2]),
                            k == 0, k == nk - 1) for k in range(nk)], tag=tag)
                    flush()
                    return
                lst = []
                for k in range(nk):
                    for t in range(3):
                        lst.append((PS[:, bset[t], 0:ncols[t]], pan[:, k, i * 128:(i + 1) * 128], rhs_fn(k, t, ncols[t]),
                                    k == 0, k == nk - 1))
                S.mms(lst, tag=tag)
                flush()

            def rhs_h(buf):
                return lambda k, t, n: buf[:, k, TT[t][0]:TT[t][0] + n]

            ada_div = [0]

            def ada_tick():
                ada_div[0] += 1
                if ada_div[0] % 3 == 0 and ada_state[0] < 22:
                    ada_round()

            ya = V(R1, 34880, BF16, [8, T])
            yb = V(R1, 34880 + 17408, BF16, [8, T])
            zbuf = V(R2, 0, F32, [4, 1122])
            convb = V(R2, 17952, F32, [4, T])
            zst = V(R2, 35360, F32, [4, 34])
            cso = V(R2, 35904, F32, [CD])[0:34]
            wcv = C["wconvT"]

            def zs_view(jj):
                return zbuf[:, jj, 1026:1122].rearrange("p (b s) -> p b s", s=6)

            for hf in range(2):
                S.copy("act", zbuf[:, :, 1026:1122].rearrange("p j (b s) -> p j b s", s=6)[:, :, :, 0:2],
                       C["scT"][:, 4 * hf:4 * hf + 4, :].rearrange("p j (b k) -> p j b k", k=2))
                for which in range(3):
                    colbase = [1024, 2048, 0][which] + hf * 512
                    for pi in range(2):
                        pan = PSt.acquire(I["w_in"], 0, KC, colbase + pi * 256, 256)
                        for i in range(2):
                            jj = 2 * pi + i
                            j = 4 * hf + jj
                            bset = next_set()
                            n2 = 364 if which < 2 else 362
                            mm_group(bset, pan, i, rhs_h(hT), KC, (363, 363, n2), tag="A%d_%d" % (which, j),
                                     split_k=(hf == 0 and which == 0 and pi == 0 and i == 0))
                            p0, p1, p2 = (PS[:, bset[0], :], PS[:, bset[1], :], PS[:, bset[2], :])
                            if which == 0:
                                S.copy("act", zbuf[:, jj, 2:365], p0[:, 0:363])
                                S.copy("act", zbuf[:, jj, 365:728], p1[:, 0:363])
                                S.copy("act", zbuf[:, jj, 728:1026], p2[:, 0:298])
                                S.copy("act", zs_view(jj)[:, :, 2:6], p2[:, 298:362].rearrange("p (b s) -> p b s", s=4))
                                S.copy("act", zbuf[:, jj, 0:2], p2[:, 362:364])
                            elif which == 1:
                                S.tt(zbuf[:, jj, 2:365], p0[:, 0:363], zbuf[:, jj, 2:365], ALU.mult)
                                S.tt(zbuf[:, jj, 365:728], p1[:, 0:363], zbuf[:, jj, 365:728], ALU.mult)
                                S.tt(zbuf[:, jj, 728:1026], p2[:, 0:298], zbuf[:, jj, 728:1026], ALU.mult)
                                S.tt(zs_view(jj)[:, :, 2:6], p2[:, 298:362].rearrange("p (b s) -> p b s", s=4),
                                     zs_view(jj)[:, :, 2:6], ALU.mult)
                                S.stt(zbuf[:, jj, 0:2], p2[:, 362:364], C["hmaskB"][:, 0:1], zbuf[:, jj, 0:2], ALU.mult, ALU.mult)
                                cb = convb[:, jj, 0:TP]
                                S.ts(cb, zbuf[:, jj, 0:TP], wcv[:, j, 0:1], None, ALU.mult)
                                S.stt(cb, zbuf[:, jj, 1:TP + 1], wcv[:, j, 1:2], cb, ALU.mult, ALU.add)
                                S.stt(cb, zbuf[:, jj, 2:TP + 2], wcv[:, j, 2:3], cb, ALU.mult, ALU.add)
                                cbs = convb[:, jj, TP:T].rearrange("p (b s) -> p b s", s=4)
                                zs = zs_view(jj)
                                S.ts(cbs, zs[:, :, 0:4], wcv[:, j, 0:1], None, ALU.mult)
                                S.stt(cbs, zs[:, :, 1:5], wcv[:, j, 1:2], cbs, ALU.mult, ALU.add)
                                S.stt(cbs, zs[:, :, 2:6], wcv[:, j, 2:3], cbs, ALU.mult, ALU.add)
                                S.copy("act", zst[:, jj, 0:2], zbuf[:, jj, 1024:1026])
                                S.copy("act", zst[:, jj, 2:34].rearrange("p (b k) -> p b k", k=2), zs[:, :, 4:6])
                                def cs_follow(jj=jj, j=j):
                                    S.transposes([(PS[0:34, 7, 0:128], zst[:, jj, :])], ident)
                                    S.copy("act", cso[:, j * 128:(j + 1) * 128], PS[0:34, 7, 0:128])
                                pend.append(cs_follow)
                            else:
                                for t in range(3):
                                    o0, n = TT[t]
                                    S.tt(ya[:, j, o0:o0 + n], PS[:, bset[t], 0:n], convb[:, jj, o0:o0 + n], ALU.mult)
                        PSt.release()
                        ada_tick()
            flush()
            S.dma("sp", O["cvo"][:, :], cso, "o_cv")

            gv = V(R2, 0, F32, [9, CD])
            vn = V(R2, 36864, BF16, [9, CD])
            vn32 = V(R2, 55296, F32, [2, CD])
            vjunk = V(R2, 63488, BF16, [CD])
            for pv in range(4):
                pan = PSt.acquire(I["w_in"], 0, KC, 4096 + pv * 256, 256)
                for c in range(9):
                    rows = 128 if c < 8 else TS
                    idx = pv * 9 + c
                    b = idx % 6
                    o = PS[0:rows, b, 0:256]
                    S.mms([(o, hT[:, k, c * 128:c * 128 + rows], pan[:, k, :], k == 0, k == KC - 1) for k in range(KC)],
                          tag="v%d_%d" % (pv, c))
                    S.act(gv[0:rows, c, pv * 256:(pv + 1) * 256], o, AF.Gelu_apprx_tanh)
                    if pv == 3:
                        S.act(vjunk[0:rows], gv[0:rows, c, :], AF.Square, accum_out=C["vss"][0:rows, c:c + 1])
                PSt.release()
                ada_tick()
            vss, vsd, vrs = C["vss"], C["vsd"], C["vrs"]
            S.act(vsd[:, 0:9], vss[:, 0:9], AF.Sqrt, bias=EPS, scale=1.0 / CD)
            S.recip(vrs[:, 0:9], vsd[:, 0:9])
            for c in range(9):
                rows = 128 if c < 8 else TS
                S.stt(vn[0:rows, c, :], gv[0:rows, c, :], vrs[0:rows, c:c + 1], gvB[0:rows], ALU.mult, ALU.mult)
            for c in (7, 8):
                rows = 128 if c < 8 else TS
                S.stt(vn32[0:rows, c - 7, :], gv[0:rows, c, :], vrs[0:rows, c:c + 1], gvB[0:rows], ALU.mult, ALU.mult)
            S.dma("sp", O["sgp"][:, :], vn32[:, 0, :], "o_sgp")
            S.dma("sp", O["sgs"][:, :], vn32[0:TS, 1, :], "o_sgs")

            gu = V(R2, 0, F32, [2, T])
            for pu in range(4):
                pan = PSt.acquire(I["w_in"], 0, KC, 3072 + pu * 256, 256)
                for i in range(2):
                    g = 2 * pu + i
                    bset = next_set()
                    mm_group(bset, pan, i, rhs_h(hT), KC, tag="u%d" % g)
                    for t in range(3):
                        o0, n = TT[t]
                        S.act(gu[:, g % 2, o0:o0 + n], PS[:, bset[t], 0:n], AF.Gelu_apprx_tanh)
                for i in range(2):
                    g = 2 * pu + i
                    sset = next_set()
                    lst = []
                    for c in range(8):
                        o = PS[:, sset[c // 4], (c % 4) * 128:(c % 4 + 1) * 128]
                        lst.append((o, vn[:, c, g * 128:(g + 1) * 128], C["WT"][:, g, :], True, False))
                        lst.append((o, bsel[:], bcomb[:, g, :], False, True))
                    o = PS[:, sset[2], 0:TS]
                    lst.append((o, vn[0:TS, 8, g * 128:(g + 1) * 128], C["Wblk"][:, g, :], True, False))
                    lst.append((o, bsel[:], bcomb_s[:, g, :], False, True))
                    S.mms(lst, tag="sp%d" % g)
                    for t in range(3):
                        o0, n = TTSP[t]
                        S.tt(yb[:, g, o0:o0 + n], PS[:, sset[t], 0:n], gu[:, g % 2, o0:o0 + n], ALU.mult)
                PSt.release()
                ada_tick()
            if STAGE < 2:
                ada_all()
                return

            merged = V(R2, 0, BF16, [KC, T])
            sab = V(R2, 34816, F32, [8, T])

            def rhs_y(buf):
                return lambda k, t, n: buf[:, k, TT[t][0]:TT[t][0] + n]

            for pp in range(8):
                par = pp % 2
                for which in range(4):
                    if which == 0:
                        pan = PSt.acquire(I["w_in"], 0, KC, 5120 + pp * 256, 256)
                    elif which == 2:
                        pan = PSt.acquire(I["w_in"], 0, KC, 7168 + pp * 256, 256)
                    elif which == 1:
                        pan = PSt.acquire(I["w_pa"], 0, 8, pp * 256, 256)
                    else:
                        pan = PSt.acquire(I["w_pb"], 0, 8, pp * 256, 256)
                    for i in range(2):
                        m = 2 * pp + i
                        bset = next_set()
                        sa = sab[:, par * 4 + i, :]
                        sbb = sab[:, par * 4 + 2 + i, :]
                        if which == 0 or which == 2:
                            mm_group(bset, pan, i, rhs_h(hT), KC, tag="g%d_%d" % (which, m))
                            dst = sa if which == 0 else sbb
                            for t in range(3):
                                o0, n = TT[t]
                                S.act(dst[:, o0:o0 + n], PS[:, bset[t], 0:n], AF.Sigmoid)
                        elif which == 1:
                            mm_group(bset, pan, i, rhs_y(ya), 8, tag="pa%d" % m)
                            for t in range(3):
                                o0, n = TT[t]
                                S.tt(sa[:, o0:o0 + n], PS[:, bset[t], 0:n], sa[:, o0:o0 + n], ALU.mult)
                        else:
                            mm_group(bset, pan, i, rhs_y(yb), 8, tag="pb%d" % m)
                            for t in range(3):
                                o0, n = TT[t]
                                S.tt(sbb[:, o0:o0 + n], PS[:, bset[t], 0:n], sbb[:, o0:o0 + n], ALU.mult)
                            S.tt(merged[:, m, :], sbb, sa, ALU.add)
                    PSt.release()
                    ada_tick()
            while ada_state[0] < 22:
                ada_round()
                flush()
            if STAGE < 3:
                ada_all()
                return

            x1 = V(R1, 0, F32, [KC, T])
            xsm = [V(R2, 34816 + i * 4608, F32, [9, 128]) for i in range(2)]
            xTsb = [V(R2, 44032 + i * 4096, F32, [TP]) for i in range(2)]
            tmp64 = C["tmp64"]
            sqf = V(R2, 52480, F32, [T])
            acc = V(R2, 56832, F32, [T])
            accb = V(R2, 61184, BF16, [T])
            gt1 = modT[:, 32:48, :]
            for m in range(KC):
                if m % 2 == 0:
                    pan = PSt.acquire(I["w_out"], 0, KC, (m // 2) * 256, 256)
                i = m % 2
                xs = xsm[m % 2]
                S.dma("sp", xs[:, 0:8, :], I["xp"][:, m * 128:(m + 1) * 128].rearrange("(c p) n -> p c n", p=128), "xsA%d" % (m % 2))
                S.dma("sp", xs[0:TS, 8, :], I["xsh"][0:TS, m * 128:(m + 1) * 128], "xsB%d" % (m % 2))
                bset = next_set()
                lst = [(PS[:, 6 + c // 4, (c % 4) * 128:(c % 4 + 1) * 128], xs[:, c, :]) for c in range(8)]
                lst.append((PS[:, bset[2], 384:384 + TS], xs[0:TS, 8, :]))
                S.transposes(lst, ident, tag="xT%d" % m)
                xT = xTsb[m % 2]
                S.copy("act", xT[:, 0:512], PS[:, 6, :])
                S.copy("act", xT[:, 512:1024], PS[:, 7, :])
                mm_group(bset, pan, i, rhs_y(merged), KC, tag="out%d" % m, split_k=(m == 0))
                S.stt(x1[:, m, 0:363], PS[:, bset[0], 0:363], gt1[:, m, 0:1], xT[:, 0:363], ALU.mult, ALU.add)
                S.stt(x1[:, m, 363:726], PS[:, bset[1], 0:363], gt1[:, m, 0:1], xT[:, 363:726], ALU.mult, ALU.add)
                S.stt(x1[:, m, 726:TP], PS[:, bset[2], 0:298], gt1[:, m, 0:1], xT[:, 726:TP], ALU.mult, ALU.add)
                S.tt(tmp64[:].rearrange("p (b s) -> p b s", s=4), PS[:, bset[2], 298:362].rearrange("p (b s) -> p b s", s=4),
                     gt1[:, m, 1:17].unsqueeze(2).broadcast_to([128, NB, 4]), ALU.mult)
                S.tt(x1[:, m, TP:T], tmp64[:], PS[:, bset[2], 384:384 + TS], ALU.add)
                if m == 0:
                    S.act(acc, x1[:, m, :], AF.Square)
                else:
                    S.act(sqf, x1[:, m, :], AF.Square)
                    S.tt(acc if m < KC - 1 else accb, acc, sqf, ALU.add)
                if m % 2 == 1:
                    PSt.release()
            if STAGE < 4:
                return

            h2T = V(R2, 0, BF16, [KC, T])
            tmpx = [V(R2, 34816 + i * 4352, F32, [T]) for i in range(3)]
            a2 = C["a2"]

            def rms_bcast(src_bf, bset_sum, bset_out):
                S.mms([(PS[:, bset_sum[t], 0:TT[t][1]], onesbf[:], src_bf[:, TT[t][0]:TT[t][0] + TT[t][1]], True, True)
                       for t in range(3)], tag="rmsum")
                for t in range(3):
                    n = TT[t][1]
                    S.act(PS[:, bset_out[t], 0:n], PS[:, bset_sum[t], 0:n], AF.Ln, bias=EPS, scale=1.0 / D)
                for t in range(3):
                    n = TT[t][1]
                    S.act(PS[:, bset_out[t], 0:n], PS[:, bset_out[t], 0:n], AF.Exp, scale=-0.5)

            s_sum, s_out = next_set(), next_set()
            rms_bcast(accb, s_sum, s_out)
            ada_all()
            txs = V(R2, 34816 + 3 * 4352, F32, [KC, TS])

            def h2_M(k):
                tx = tmpx[k % 3]
                S.tt(tx[:, 0:363], PS[:, s_out[0], 0:363], x1[:, k, 0:363], ALU.mult)
                S.tt(tx[:, 363:726], PS[:, s_out[1], 0:363], x1[:, k, 363:726], ALU.mult)
                S.tt(tx[:, 726:TP], PS[:, s_out[2], 0:298], x1[:, k, 726:TP], ALU.mult)
                S.tt(txs[:, k, :], PS[:, s_out[2], 298:362], x1[:, k, TP:T], ALU.mult)
                S.act(h2T[:, k, 0:TP], tx[:, 0:TP], AF.Identity, bias=modT[:, 48 + k, 0:1], scale=a2[:, k, 0:1])

            for k in range(KC):
                h2_M(k)
            txs4 = txs.rearrange("p k (b s) -> p k b s", s=4)
            S.tt(txs4, txs4, a2[:, :, 1:17].unsqueeze(3).broadcast_to([128, KC, NB, 4]), ALU.mult)
            S.tt(h2T[:, :, TP:T].rearrange("p k (b s) -> p k b s", s=4), txs4,
                 modT[:, 48:64, 1:17].unsqueeze(3).broadcast_to([128, KC, NB, 4]), ALU.add)

            if STAGE < 5:
                return
            actb = V(R2, 34816, BF16, [16, T])
            sg = V(MISC, 0, F32, [2, T])
            gt2 = modT[:, 80:96, :]
            for (j0, j1) in [(0, 16), (16, 32), (32, 44)]:
                for j in range(j0, j1, 2):
                    pg = PSt.acquire(I["w_ffn_in"], 0, KC, j * 128, 256)
                    pu_ = PSt.acquire(I["w_ffn_in"], 0, KC, DFF + j * 128, 256)
                    for i in range(2):
                        bset = next_set()
                        mm_group(bset, pg, i, rhs_y(h2T), KC, tag="gate%d" % (j + i), split_k=(j == 0 and i == 0))
                        for t in range(3):
                            o0, n = TT[t]
                            S.act(sg[:, i, o0:o0 + n], PS[:, bset[t], 0:n], AF.Silu)
                        bset = next_set()
                        mm_group(bset, pu_, i, rhs_y(h2T), KC, tag="up%d" % (j + i))
                        for t in range(3):
                            o0, n = TT[t]
                            S.tt(actb[:, j - j0 + i, o0:o0 + n], PS[:, bset[t], 0:n], sg[:, i, o0:o0 + n], ALU.mult)
                    PSt.release()
                    PSt.release()
                nk = j1 - j0
                for mp in range(8):
                    pan = PSt.acquire(I["w_ffn_out"], j0 * 128, nk, mp * 256, 256)
                    for i in range(2):
                        m = 2 * mp + i
                        bset = next_set()
                        mm_group(bset, pan, i, rhs_y(actb), nk, tag="fo%d_%d" % (j0, m))
                        S.stt(x1[:, m, 0:363], PS[:, bset[0], 0:363], gt2[:, m, 0:1], x1[:, m, 0:363], ALU.mult, ALU.add)
                        S.stt(x1[:, m, 363:726], PS[:, bset[1], 0:363], gt2[:, m, 0:1], x1[:, m, 363:726], ALU.mult, ALU.add)
                        S.stt(x1[:, m, 726:TP], PS[:, bset[2], 0:298], gt2[:, m, 0:1], x1[:, m, 726:TP], ALU.mult, ALU.add)
                        t64 = tmp64[:].rearrange("p (b s) -> p b s", s=4)
                        S.tt(t64, PS[:, bset[2], 298:362].rearrange("p (b s) -> p b s", s=4),
                             gt2[:, m, 1:17].unsqueeze(2).broadcast_to([128, NB, 4]), ALU.mult)
                        S.tt(x1[:, m, TP:T], x1[:, m, TP:T], tmp64[:], ALU.add)
                    PSt.release()

            if STAGE < 6:
                return
            yst = [V(R2, i * 8192, F32, [D]) for i in range(2)]
            gfinB = V(R2, 16384, F32, [D])
            S.dma("sp", gfinB, bass.AP(I["g_final"].tensor, 0, [[0, 128], [1, D]]), "p_gf")
            fss, fsd, frs = C["fss"], C["fsd"], C["frs"]
            fjunk = V(R2, 24576, BF16, [D])
            s_old = 3 * (set_ctr[0] % 2)
            s_new = 3 * ((set_ctr[0] + 1) % 2)
            fbanks = [[s_old, s_old + 1, s_old + 2, 6], [s_new, s_new + 1, s_new + 2, 7]]
            for c in range(9):
                rows = 128 if c < 8 else TS
                bk = fbanks[c % 2]
                S.transposes([(PS[0:rows, bk[m // 4], (m % 4) * 128:(m % 4 + 1) * 128], x1[:, m, c * 128:c * 128 + rows])
                              for m in range(KC)], ident, tag="yT%d" % c)
                fq = C["fq"][:, 4 * (c % 2):4 * (c % 2) + 4]
                for q in range(4):
                    S.act(fjunk[0:rows, q * 512:(q + 1) * 512], PS[0:rows, bk[q], :], AF.Square, accum_out=fq[0:rows, q:q + 1])
                S.add("dve", lambda e, c=c, rows=rows, fq=fq: e.tensor_reduce(out=fss[0:rows, c:c + 1], in_=fq[0:rows, 0:4],
                                                                              axis=mybir.AxisListType.X, op=ALU.add),
                      [fq[0:rows, 0:4]], [fss[0:rows, c:c + 1]])
                S.act(fsd[0:rows, c:c + 1], fss[0:rows, c:c + 1], AF.Sqrt, bias=EPS, scale=1.0 / D)
                S.recip(frs[0:rows, c:c + 1], fsd[0:rows, c:c + 1])
                y = yst[c % 2][0:rows]
                for q in range(4):
                    S.stt(y[:, q * 512:(q + 1) * 512], PS[0:rows, bk[q], :], frs[0:rows, c:c + 1],
                          gfinB[0:rows, q * 512:(q + 1) * 512], ALU.mult, ALU.mult)
                if c < 8:
                    S.dma("sp", O["yp"][c * 128:(c + 1) * 128, :], y, "o_y%d" % (c % 2))
                else:
                    S.dma("sp", O["ys"][:, :], y, "o_y%d" % (c % 2))

        S0 = Sched()
        P0 = PanelStream(S0, ring, plan=None)
        construct(S0, P0)
        S1 = Sched()
        P1 = PanelStream(S1, ring, plan=P0.rec)
        construct(S1, P1)
        assert P1.next_use == len(P0.rec)
        S1.emit(nc)
    return nc


_NC_CACHE = {}


def kernel(x_prompt, x_sample, state_conv, c_prompt, c_sample, g_mix, g_ffn, w_ada, b_ada, w_in, w_conv, g_v,
           w_sg, b_sg, w_pa, w_pb, w_out, w_ffn_in, w_ffn_out, g_final):
    f = lambda a: np.ascontiguousarray(np.asarray(a, dtype=np.float32))
    x_prompt, x_sample, state_conv = f(x_prompt), f(x_sample), f(state_conv)
    c_prompt, c_sample = f(c_prompt), f(c_sample)
    shared = dict(
        w_ada=f(w_ada)[0], b_ada=f(b_ada)[0].reshape(96, 128), w_in=f(w_in)[0], w_conv=f(w_conv)[0],
        g_v=f(g_v)[0].reshape(1, CD), w_sg=f(w_sg)[0], b_sg=f(b_sg)[0].reshape(1, 1024),
        w_pa=f(w_pa)[0], w_pb=f(w_pb)[0], w_out=f(w_out)[0], w_ffn_in=f(w_ffn_in)[0], w_ffn_out=f(w_ffn_out)[0],
        g_mix=f(g_mix)[0].reshape(1, D), g_ffn=f(g_ffn)[0].reshape(1, D), g_final=f(g_final).reshape(1, D),
    )
    in_maps = []
    for i in range(NCORES):
        b, half = i // 2, i % 2
        t0 = half * TP
        xs = x_sample[NB * i:NB * (i + 1)].reshape(TS, D)
        if half == 1:
            halo = x_prompt[b, t0 - 2:t0]
            hm = np.ones((1, 1), np.float32)
        else:
            halo = np.zeros((2, D), np.float32)
            hm = np.zeros((1, 1), np.float32)
        m = dict(shared)
        m.update(
            xp=np.ascontiguousarray(x_prompt[b, t0:t0 + TP]),
            xsh=np.ascontiguousarray(np.concatenate([xs, halo], axis=0)),
            hmask=hm,
            sc=np.ascontiguousarray(state_conv[0, NB * i:NB * (i + 1)].reshape(2 * NB, CD)),
            cvec=np.ascontiguousarray(np.concatenate([c_prompt[b:b + 1], c_sample[NB * i:NB * (i + 1)]], axis=0)),
        )
        in_maps.append(m)
    if "nc" not in _NC_CACHE:
        _NC_CACHE["nc"] = build_nc()
    res = run_bass_kernel_spmd(_NC_CACHE["nc"], in_maps, core_ids=list(range(NCORES)))
    R = res.results
    B = x_prompt.shape[0]
    y_prompt = np.zeros((B, 2 * TP, D), np.float32)
    y_sample = np.zeros((NB * NCORES, 4, D), np.float32)
    conv_p = np.zeros((1, B, 2, CD), np.float32)
    conv_s = np.zeros((1, NB * NCORES, 2, CD), np.float32)
    sgv_p = np.zeros((1, B, 128, 8, 128), np.float32)
    sgv_s = np.zeros((1, NB * NCORES, 4, 8, 128), np.float32)
    for i in range(NCORES):
        b, half = i // 2, i % 2
        r = R[i]
        y_prompt[b, half * TP:(half + 1) * TP] = r["yp"]
        y_sample[NB * i:NB * (i + 1)] = r["ys"].reshape(NB, 4, D)
        conv_s[0, NB * i:NB * (i + 1)] = r["cvo"][2:].reshape(NB, 2, CD)
        sgv_s[0, NB * i:NB * (i + 1)] = r["sgs"].reshape(NB, 4, 8, 128)
        if half == 1:
            conv_p[0, b] = r["cvo"][0:2]
            sgv_p[0, b] = r["sgp"].reshape(128, 8, 128)
    _NC_CACHE["last"] = R
    return (y_prompt, y_sample, conv_p, conv_s, sgv_p, sgv_s)
```

```python
import contextlib
import os
import numpy as np
import concourse.bass as bass
import concourse.mybir as mybir
from concourse.bass_utils import run_bass_kernel_spmd

F32 = mybir.dt.float32
BF16 = mybir.dt.bfloat16
AF = mybir.ActivationFunctionType
ALU = mybir.AluOpType

NCORES = 8
D = 2048
KC = 16
TP = 1024
TS = 64
T = TP + TS
TH = T + 2
NB = 16
NR = 18
CD = 1024
DFF = 5632
NFF = DFF // 128
EPS = 1e-6
IN_COLS = 9216
TT = [(0, 363), (363, 363), (726, 362)]
TTSP = [(0, 512), (512, 512), (1024, 64)]
ENGS = ["pe", "act", "dve", "pool", "sp"]
STAGE = int(os.environ.get("KSTAGE", "9"))
DEBUG = int(os.environ.get("KDEBUG", "0"))


class Op:
    __slots__ = ("eng", "fn", "deps", "is_dma", "dsem", "dcount", "needs_inc", "event", "tag")

    def __init__(self, eng, fn, is_dma, dsem, tag):
        self.eng = eng
        self.fn = fn
        self.deps = []
        self.is_dma = is_dma
        self.dsem = dsem
        self.dcount = 0
        self.needs_inc = False
        self.event = 0
        self.tag = tag


def ap_ranges(ap):
    sp = str(ap.space)
    if sp == "DRAM":
        return []
    pst = ap.ap[0][0]
    esz = mybir.dt.size(ap.dtype)
    off = ap.offset % pst if pst > 0 else ap.offset
    dims = list(ap.ap[1:])
    name = ap.tensor.name

    def ext_of(ds):
        e = 1
        for (st, cnt) in ds:
            e += (cnt - 1) * abs(st)
        return e

    outs = []
    if len(dims) >= 2 and 1 < dims[0][1] <= 64 and abs(dims[0][0]) > ext_of(dims[1:]):
        inner = ext_of(dims[1:])
        for i in range(dims[0][1]):
            lo = off + i * dims[0][0]
            outs.append((lo * esz, (lo + inner) * esz))
    else:
        outs.append((off * esz, (off + ext_of(dims)) * esz))
    res = []
    for (lo, hi) in outs:
        if sp == "PSUM":
            lo = lo // 2048 * 2048
            hi = -(-hi // 2048) * 2048
        res.append((name, lo, hi))
    return res


class Sched:
    def __init__(self):
        self.ops = {e: [] for e in ENGS}
        self.ent = {}
        self.dma_counts = {}

    def add(self, eng, fn, reads=(), writes=(), dma=None, tag=""):
        op = Op(eng, fn, dma is not None, dma, tag)
        deps = []

        def need(o, kind):
            if o is None or o is op:
                return
            if o.eng != eng or o.is_dma or op.is_dma:
                deps.append(o)
            elif eng != "pe":
                deps.append(o)

        rd_ranges = [r for ap in reads for r in ap_ranges(ap)]
        wr_ranges = [r for ap in writes for r in ap_ranges(ap)]
        for (name, lo, hi) in rd_ranges:
            for e in self.ent.get(name, ()):
                if e[0] < hi and lo < e[1]:
                    need(e[2], "raw")
                    e[3].append(op)
        for (name, lo, hi) in wr_ranges:
            lst = self.ent.setdefault(name, [])
            new = []
            for e in lst:
                if e[0] < hi and lo < e[1]:
                    need(e[2], "waw")
                    for rd in e[3]:
                        need(rd, "war")
                    if e[0] < lo:
                        new.append([e[0], lo, e[2], list(e[3])])
                    if e[1] > hi:
                        new.append([hi, e[1], e[2], list(e[3])])
                else:
                    new.append(e)
            new.append([lo, hi, op, []])
            self.ent[name] = new
        seen = set()
        for d in deps:
            if id(d) not in seen:
                seen.add(id(d))
                op.deps.append(d)
                if not d.is_dma:
                    d.needs_inc = True
        if op.is_dma:
            self.dma_counts[dma] = self.dma_counts.get(dma, 0) + 1
            op.dcount = self.dma_counts[dma]
        self.ops[eng].append(op)
        return op

    def emit(self, nc):
        for e in ENGS:
            c = 0
            for op in self.ops[e]:
                if op.needs_inc:
                    c += 1
                    op.event = c
        with contextlib.ExitStack() as st:
            esem = {e: st.enter_context(nc.semaphore("e_" + e)) for e in ENGS}
            dsem = {n: st.enter_context(nc.semaphore("d_" + n)) for n in self.dma_counts}
            block = st.enter_context(nc.Block())
            sched = self

            def body(ename):
                def _f(eng):
                    waited = {}
                    for op in sched.ops[ename]:
                        for d in op.deps:
                            if d.is_dma:
                                key, val, sem = ("d", d.dsem), 16 * d.dcount, dsem[d.dsem]
                            else:
                                key, val, sem = ("e", d.eng), d.event, esem[d.eng]
                            if waited.get(key, 0) >= val:
                                continue
                            waited[key] = val
                            eng.wait_ge(sem, val)
                        ins = op.fn(eng)
                        if op.is_dma:
                            ins.then_inc(dsem[op.dsem], 16)
                        elif op.needs_inc:
                            ins.then_inc(esem[ename], 1)
                    if ename == "sp":
                        for n, c in sched.dma_counts.items():
                            eng.wait_ge(dsem[n], 16 * c)
                return _f

            block.tensor(body("pe"))
            block.scalar(body("act"))
            block.vector(body("dve"))
            block.gpsimd(body("pool"))
            block.sync(body("sp"))

    def act(self, out, in_, func, bias=None, scale=None, accum_out=None, tag=""):
        kw = {}
        rd = [in_]
        wr = [out]
        if bias is not None:
            kw["bias"] = bias
            if not isinstance(bias, (int, float)):
                rd.append(bias)
        if scale is not None:
            kw["scale"] = scale
            if not isinstance(scale, (int, float)):
                rd.append(scale)
        if accum_out is not None:
            kw["accum_out"] = accum_out
            wr.append(accum_out)
        return self.add("act", lambda e: e.activation(out=out, in_=in_, func=func, **kw), rd, wr, tag=tag)

    def tt(self, out, in0, in1, op, tag=""):
        return self.add("dve", lambda e: e.tensor_tensor(out=out, in0=in0, in1=in1, op=op), [in0, in1], [out], tag=tag)

    def stt(self, out, in0, scalar, in1, op0, op1, tag=""):
        rd = [in0, in1] + ([] if isinstance(scalar, (int, float)) else [scalar])
        return self.add("dve", lambda e: e.scalar_tensor_tensor(out=out, in0=in0, scalar=scalar, in1=in1, op0=op0, op1=op1),
                        rd, [out], tag=tag)

    def ts(self, out, in0, s1, s2, op0, op1=None, tag=""):
        rd = [in0] + [s for s in (s1, s2) if s is not None and not isinstance(s, (int, float))]
        if op1 is None:
            return self.add("dve", lambda e: e.tensor_scalar(out=out, in0=in0, scalar1=s1, scalar2=None, op0=op0), rd, [out], tag=tag)
        return self.add("dve", lambda e: e.tensor_scalar(out=out, in0=in0, scalar1=s1, scalar2=s2, op0=op0, op1=op1), rd, [out], tag=tag)

    def copy(self, eng, out, in_, tag=""):
        if eng == "act":
            return self.add("act", lambda e: e.copy(out=out, in_=in_), [in_], [out], tag=tag)
        return self.add(eng, lambda e: e.tensor_copy(out=out, in_=in_), [in_], [out], tag=tag)

    def recip(self, out, in_):
        return self.add("dve", lambda e: e.reciprocal(out=out, in_=in_), [in_], [out])

    def mms(self, lst, tag=""):
        rd, wr = [], []
        for (o, l, r, s0, s1) in lst:
            rd += [l, r]
            wr.append(o)

        def fn(e):
            ins = None
            for (o, l, r, s0, s1) in lst:
                ins = e.matmul(o, lhsT=l, rhs=r, start=s0, stop=s1)
            return ins
        return self.add("pe", fn, rd, wr, tag=tag)

    def transposes(self, lst, ident, tag=""):
        rd, wr = [], []
        for (o, i) in lst:
            rd.append(i)
            wr.append(o)

        def fn(e):
            ins = None
            for (o, i) in lst:
                k = i.ap[0][1]
                ins = e.transpose(out=o, in_=i, identity=ident[0:k, 0:k])
            return ins
        return self.add("pe", fn, rd + [ident[:]], wr, tag=tag)

    def dma(self, eng, out, in_, sem, slow=False, tag=""):
        if slow:
            f = lambda e: e.dma_start(out=out, in_=in_, allow_slow_non_contiguous=True)
        else:
            f = lambda e: e.dma_start(out=out, in_=in_)
        return self.add(eng, f, [in_], [out], dma=sem, tag=tag)


class PanelStream:
    NSLOT = 4

    def __init__(self, S, ring, plan=None):
        self.S = S
        self.ring = ring
        self.plan = plan
        self.rec = []
        self.next_load = 0
        self.next_use = 0
        self.loaded_slots = 0
        self.freed_slots = 0
        self.use_slot = 0
        self.pending_free = []
    def start(self):
        if self.plan is not None:
            self._fill()

    @staticmethod
    def _width(nk, ncols):
        return 1 if nk * ncols <= 4096 else 2

    def _view(self, slot0, nk, ncols):
        wd = self._width(nk, ncols)
        slot = slot0 % self.NSLOT
        if wd == 1:
            return self.ring[:, slot, 0:nk * ncols].rearrange("p (k n) -> p k n", n=ncols)
        assert slot % 2 == 0
        return self.ring[:, slot:slot + 2, :].rearrange("p s n -> p (s n)")[:, 0:nk * ncols].rearrange("p (k n) -> p k n", n=ncols)

    def _fill(self):
        while self.next_load < len(self.plan):
            (w, r0, nk, c0, ncols) = self.plan[self.next_load]
            wd = self._width(nk, ncols)
            if self.loaded_slots - self.freed_slots + wd > self.NSLOT:
                return
            src = w[r0:r0 + nk * 128, c0:c0 + ncols].rearrange("(k p) n -> p k n", p=128)
            dst = self._view(self.loaded_slots, nk, ncols)
            self.S.dma("pool", dst, src, "ring%d" % (self.loaded_slots % self.NSLOT), tag="panel%d" % self.next_load)
            self.loaded_slots += wd
            self.next_load += 1

    def acquire(self, w, r0, nk, c0, ncols):
        i = self.next_use
        self.next_use += 1
        if self.plan is None:
            self.rec.append((w, r0, nk, c0, ncols))
        else:
            assert self.plan[i][1:] == (r0, nk, c0, ncols), (i, self.plan[i][1:], (r0, nk, c0, ncols))
        v = self._view(self.use_slot, nk, ncols)
        wd = self._width(nk, ncols)
        self.use_slot += wd
        self.pending_free.append(wd)
        return v

    def release(self):
        wd = self.pending_free.pop(0)
        self.freed_slots += wd
        if self.plan is not None:
            self._fill()


def V(arena, boff, dtype, dims):
    esz = mybir.dt.size(dtype)
    n = 1
    for d in dims:
        n *= d
    assert boff % 4 == 0 and (n * esz) % 4 == 0
    ap = arena[:, boff // 4:(boff + n * esz) // 4]
    if dtype != F32:
        ap = ap.bitcast(dtype)
    if len(dims) == 2:
        ap = ap.rearrange("p (a b) -> p a b", b=dims[1])
    elif len(dims) == 3:
        ap = ap.rearrange("p (a b c) -> p a b c", b=dims[1], c=dims[2])
    return ap


def build_nc():
    nc = bass.Bass("TRN2", target_bir_lowering=False)

    def din(name, shape):
        return nc.dram_tensor(name, list(shape), F32, kind="ExternalInput").ap()

    def dout(name, shape):
        return nc.dram_tensor(name, list(shape), F32, kind="ExternalOutput").ap()

    I = dict(
        xp=din("xp", [TP, D]), xsh=din("xsh", [TS + 2, D]), hmask=din("hmask", [1, 1]),
        sc=din("sc", [2 * NB, CD]), cvec=din("cvec", [NB + 1, D]),
        w_ada=din("w_ada", [D, 6 * D]), b_ada=din("b_ada", [96, 128]),
        w_in=din("w_in", [D, IN_COLS]), w_conv=din("w_conv", [3, CD]), g_v=din("g_v", [1, CD]),
        w_sg=din("w_sg", [8, 128, 128]), b_sg=din("b_sg", [1, 1024]),
        w_pa=din("w_pa", [CD, D]), w_pb=din("w_pb", [CD, D]), w_out=din("w_out", [D, D]),
        w_ffn_in=din("w_ffn_in", [D, 2 * DFF]), w_ffn_out=din("w_ffn_out", [DFF, D]),
        g_mix=din("g_mix", [1, D]), g_ffn=din("g_ffn", [1, D]), g_final=din("g_final", [1, D]),
    )
    O = dict(
        yp=dout("yp", [TP, D]), ys=dout("ys", [TS, D]), cvo=dout("cvo", [2 + 2 * NB, CD]),
        sgp=dout("sgp", [128, CD]), sgs=dout("sgs", [TS, CD]),
    )
    if DEBUG:
        O["dbg"] = dout("dbg", [128, 8192])

    with contextlib.ExitStack() as st:
        def sb(name, shape, dt=F32):
            return st.enter_context(nc.sbuf_tensor(name, list(shape), dt))

        R1 = sb("R1", [128, 17440])
        R2 = sb("R2", [128, 17440])
        MISC = sb("MISC", [128, 2176])
        ring = sb("ring", [128, 4, 4096], BF16)
        PS = st.enter_context(nc.psum_tensor("PS", [128, 8, 512], F32))
        C = dict(
            ident=sb("ident", [128, 128]), cmask=sb("cmask", [128, 128]), bmask=sb("bmask", [64, 64]), W4T=sb("W4T", [4, 8, 4]), E4=sb("E4", [4, 64]),
            ones1=sb("ones1", [1, 128]), onesbf=sb("onesbf", [128, 128], BF16),
            bcomb=sb("bcomb", [128, 8, 128], BF16), bcomb_s=sb("bcomb_s", [128, 8, 64], BF16), bsel=sb("bsel", [128, 128], BF16),
            gT=sb("gT", [128, 32]), gnat=sb("gnat", [32, 128]), wconvT=sb("wconvT", [128, 8, 3]),
            hmaskB=sb("hmaskB", [128, 1]), bT=sb("bT", [128, 96]), bnat=sb("bnat", [96, 128]),
            csilT=sb("csilT", [128, 16, NR], BF16), modtok=sb("modtok", [NR, 512]),
            modT=sb("modT", [128, 96, NR]), a1=sb("a1", [128, 16, NR]), a2=sb("a2", [128, 16, NR]),
            WT=sb("WT", [128, 8, 128], BF16), Wblk=sb("Wblk", [64, 8, 64], BF16), scT=sb("scT", [128, 8, 32]),
            ss=sb("ss", [128, 16]), sd=sb("sd", [128, 16]), rstd=sb("rstd", [128, 16]),
            vss=sb("vss", [128, 16]), vsd=sb("vsd", [128, 16]), vrs=sb("vrs", [128, 16]),
            fss=sb("fss", [128, 16]), fq=sb("fq", [128, 8]), tmp64=sb("tmp64", [128, TS]), fsd=sb("fsd", [128, 16]), frs=sb("frs", [128, 16]),
        )

        def construct(S, PSt):
            ident = C["ident"]
            modT = C["modT"]
            bank_ctr = [0]
            set_ctr = [0]

            def next_set():
                s = set_ctr[0] % 2
                set_ctr[0] += 1
                return (3 * s, 3 * s + 1, 3 * s + 2)

            S.add("pool", lambda e: e.memset(ident[:], 1.0), [], [ident[:]])
            S.add("pool", lambda e: e.affine_select(out=ident[:], in_=ident[:], pattern=[[-1, 128]], base=0,
                                                      channel_multiplier=1, compare_op=ALU.is_equal, fill=0.0),
                  [ident[:]], [ident[:]])
            cst = V(R2, 0, F32, [D])[0:NR]
            S.add("dve", lambda e: e.memset(cst, 0.0), [], [cst])
            for qq in range(4):
                S.dma("sp", cst[0:NB + 1, qq * 512:(qq + 1) * 512], I["cvec"][:, qq * 512:(qq + 1) * 512], "p_c%d" % qq)
            PSt.start()
            cmask = C["cmask"]
            S.add("pool", lambda e: e.memset(cmask[:], 1.0), [], [cmask[:]])
            S.add("pool", lambda e: e.affine_select(out=cmask[:], in_=cmask[:], pattern=[[1, 128]], base=0,
                                                      channel_multiplier=-1, compare_op=ALU.is_ge, fill=0.0),
                  [cmask[:]], [cmask[:]])
            ones1, onesbf = C["ones1"], C["onesbf"]
            S.add("pool", lambda e: e.memset(C["vss"][:], 1.0), [], [C["vss"][:]])
            S.add("pool", lambda e: e.memset(ones1[:], 1.0), [], [ones1[:]])
            bcomb, bcomb_s, bsel = C["bcomb"], C["bcomb_s"], C["bsel"]
            S.add("pool", lambda e: e.memset(bcomb[:], 0.0), [], [bcomb[:]])
            S.add("pool", lambda e: e.memset(bsel[:], 0.0), [], [bsel[:]])
            S.add("pool", lambda e: e.memset(bsel[0:1, :], 1.0), [bsel[:]], [bsel[0:1, :]])
            S.add("pool", lambda e: e.memset(bsel[32:33, :], 1.0), [bsel[:]], [bsel[32:33, :]])
            S.add("pool", lambda e: e.memset(onesbf[:], 1.0), [], [onesbf[:]])
            bmask = C["bmask"]
            E4 = C["E4"]
            S.add("pool", lambda e: e.memset(E4[:], 1.0), [], [E4[:]])
            S.add("pool", lambda e: e.affine_select(out=E4[:], in_=E4[:], pattern=[[0, 16], [-1, 4]], base=0,
                                                      channel_multiplier=1, compare_op=ALU.is_equal, fill=0.0),
                  [E4[:]], [E4[:]])
            S.add("pool", lambda e: e.memset(bmask[:], 1.0), [], [bmask[:]])
            S.add("pool", lambda e: e.affine_select(out=bmask[:], in_=bmask[:], pattern=[[1, 64]], base=0,
                                                      channel_multiplier=-1, compare_op=ALU.is_ge, fill=0.0),
                  [bmask[:]], [bmask[:]])
            S.add("pool", lambda e: e.affine_select(out=bmask[:], in_=bmask[:], pattern=[[-4, 16], [0, 4]], base=0,
                                                      channel_multiplier=1, compare_op=ALU.is_ge, fill=0.0),
                  [bmask[:]], [bmask[:]])

            S.dma("sp", C["bnat"][:], I["b_ada"][:, :], "p_b")
            gnat = C["gnat"]
            S.dma("sp", gnat[0:16, :], I["g_mix"].rearrange("o (k p) -> (o k) p", p=128), "p_g1")
            S.dma("sp", gnat[16:32, :], I["g_ffn"].rearrange("o (k p) -> (o k) p", p=128), "p_g2")
            sct = V(MISC, 4096, F32, [CD])[0:2 * NB]
            S.dma("sp", sct, I["sc"][:, :], "p_sc")

            S.act(cst, cst, AF.Silu)
            S.transposes([(PS[:, 6, k * NR:(k + 1) * NR], cst[:, k * 128:(k + 1) * 128]) for k in range(KC)], ident)
            S.copy("dve", C["csilT"][:], PS[:, 6, 0:KC * NR].rearrange("p (k r) -> p k r", r=NR))
            S.transposes([(PS[:, 7, 0:96], C["bnat"][:]), (PS[:, 7, 96:128], C["gnat"][:])], ident)
            S.copy("dve", C["bT"][:], PS[:, 7, 0:96])
            S.copy("dve", C["gT"][:], PS[:, 7, 96:128])

            ada_state = [0]
            pend = []

            def flush():
                for f in pend:
                    f()
                del pend[:]

            def ada_round():
                r = ada_state[0]
                if r >= 24:
                    return
                ada_state[0] += 1
                if r < 8:
                    pan = PSt.acquire(I["w_ada"], 0, KC, r * 512, 512)
                    S.mms([(PS[0:NR, 6, :], C["csilT"][:, k, :], pan[:, k, :], k == 0, k == KC - 1)
                           for k in range(KC)], tag="ada%d" % r)
                    PSt.release()
                else:
                    for half in range(2):
                        pan = PSt.acquire(I["w_ada"], 0, KC, r * 512 + half * 256, 256)
                        S.mms([(PS[0:NR, 6, half * 256:(half + 1) * 256], C["csilT"][:, k, :], pan[:, k, :], k == 0, k == KC - 1)
                               for k in range(KC)], tag="ada%d" % r)
                        PSt.release()
                mt = C["modtok"]
                S.copy("act", mt[:], PS[0:NR, 6, :])

                def follow(r=r, mt=mt):
                    S.transposes([(PS[:, 7, kk * NR:(kk + 1) * NR], mt[:, kk * 128:(kk + 1) * 128]) for kk in range(4)], ident)
                    S.tt(modT[:, 4 * r:4 * r + 4, :], PS[:, 7, 0:4 * NR].rearrange("p (k r) -> p k r", r=NR),
                         C["bT"][:, 4 * r:4 * r + 4].unsqueeze(2).broadcast_to([128, 4, NR]), ALU.add)
                    if r == 7:
                        S.ts(C["a1"][:], modT[:, 16:32, :], 1.0, None, ALU.add)
                        S.tt(C["a1"][:], C["a1"][:], C["gT"][:, 0:16].unsqueeze(2).broadcast_to([128, 16, NR]), ALU.mult)
                    if r == 19:
                        S.ts(C["a2"][:], modT[:, 64:80, :], 1.0, None, ALU.add)
                        S.tt(C["a2"][:], C["a2"][:], C["gT"][:, 16:32].unsqueeze(2).broadcast_to([128, 16, NR]), ALU.mult)
                pend.append(follow)

            def ada_all():
                while ada_state[0] < 24:
                    ada_round()
                    flush()

            xnT = V(R2, 0, F32, [KC, TH])
            hT = V(R1, 0, BF16, [KC, TH])
            xt = [V(R1, 34880 + i * 8192, F32, [D]) for i in range(2)]
            xn = [V(R1, 34880 + 16384 + i * 8192, F32, [D]) for i in range(2)]
            ss, sd, rstd = C["ss"], C["sd"], C["rstd"]
            def p0_stage1(c):
                rows = 128 if c < 8 else TS + 2
                src = I["xp"][c * 128:(c + 1) * 128, :] if c < 8 else I["xsh"][:, :]
                xt_c = xt[c % 2][0:rows]
                xn_c = xn[c % 2][0:rows]
                S.dma("sp", xt_c, src, "xt%d" % (c % 2))
                S.act(xn_c, xt_c, AF.Square, accum_out=ss[0:rows, c:c + 1])
                S.act(sd[0:rows, c:c + 1], ss[0:rows, c:c + 1], AF.Sqrt, bias=EPS, scale=1.0 / D)
                S.recip(rstd[0:rows, c:c + 1], sd[0:rows, c:c + 1])
                S.ts(xn_c, xt_c, rstd[0:rows, c:c + 1], None, ALU.mult)

            def p0_stage2(c):
                rows = 128 if c < 8 else TS + 2
                xn_c = xn[c % 2][0:rows]
                for q in range(4):
                    b = bank_ctr[0] % 6
                    bank_ctr[0] += 1
                    S.transposes([(PS[:, b, kk * 128:kk * 128 + rows], xn_c[:, (4 * q + kk) * 128:(4 * q + kk + 1) * 128])
                                  for kk in range(4)], ident)
                    src_ps = PS[:, b, :].rearrange("p (k t) -> p k t", t=128)[:, :, 0:rows]
                    dst = xnT[:, 4 * q:4 * q + 4, c * 128:c * 128 + rows]
                    S.copy("act" if q % 2 == 0 else "dve", dst, src_ps)
                flush()
                if c < 8:
                    ada_round()

            p0_stage1(0)
            for c in range(9):
                if c + 1 < 9:
                    p0_stage1(c + 1)
                p0_stage2(c)
            flush()
            wsg_nat = V(MISC, 0, F32, [8, 128])
            S.dma("sp", wsg_nat, I["w_sg"].rearrange("g t s -> t g s"), "p_wsg")
            wc = I["w_conv"]
            for kk in range(3):
                S.dma("sp", C["wconvT"][:, :, kk], bass.AP(wc.tensor, kk * CD, [[1, 128], [128, 8]]), "p_wc", slow=True)
            S.dma("sp", C["hmaskB"][:], bass.AP(I["hmask"].tensor, 0, [[0, 128], [1, 1]]), "p_hm")
            wsg = I["w_sg"]
            W4T = C["W4T"]
            for t4 in range(4):
                S.dma("sp", W4T[:, :, t4], bass.AP(wsg.tensor, t4 * 128, [[1, 4], [128 * 128, 8]]), "p_w4", slow=True)
            a1 = C["a1"]
            xs_all = xnT[:, :, TP:T].rearrange("p k (b s) -> p k b s", s=4)
            S.tt(xs_all, xs_all, a1[:, :, 1:17].unsqueeze(3).broadcast_to([128, KC, NB, 4]), ALU.mult)
            xh_all = xnT[:, :, T:TH]
            S.tt(xh_all, xh_all, a1[:, :, 0:1].broadcast_to([128, KC, 2]), ALU.mult)
            for k in range(KC):
                if k % 2 == 0:
                    S.act(hT[:, k, 0:TP], xnT[:, k, 0:TP], AF.Identity, bias=modT[:, k, 0:1], scale=a1[:, k, 0:1])
                else:
                    S.ts(hT[:, k, 0:TP], xnT[:, k, 0:TP], a1[:, k, 0:1], modT[:, k, 0:1], ALU.mult, ALU.add)
            S.tt(hT[:, :, TP:T].rearrange("p k (b s) -> p k b s", s=4), xs_all,
                 modT[:, 0:KC, 1:17].unsqueeze(3).broadcast_to([128, KC, NB, 4]), ALU.add)
            S.tt(hT[:, :, T:TH], xh_all, modT[:, 0:KC, 0:1].broadcast_to([128, KC, 2]), ALU.add)

            btmp = V(MISC, 0, F32, [1024])
            bhi = V(MISC, 4096, BF16, [1024])

            def sg_setup():
                for hh in range(2):
                    S.transposes([(PS[:, 6 + hh, gg * 128:(gg + 1) * 128], wsg_nat[:, 4 * hh + gg, :]) for gg in range(4)], ident)
                    S.tt(C["WT"][:, 4 * hh:4 * hh + 4, :], PS[:, 6 + hh, :].rearrange("p (g t) -> p g t", t=128),
                         cmask[:].unsqueeze(1).broadcast_to([128, 4, 128]), ALU.mult)
                S.mms([(PS[0:TS, 6, 0:32], C["E4"][:], W4T[:].rearrange("p g t -> p (g t)"), True, True)])
                S.tt(C["Wblk"][:].rearrange("p g (b t) -> p g b t", t=4),
                     PS[0:TS, 6, 0:32].rearrange("p (g t) -> p g t", t=4).unsqueeze(2).broadcast_to([TS, 8, NB, 4]),
                     bmask[:].rearrange("p (b t) -> p b t", t=4).unsqueeze(1).broadcast_to([TS, 8, NB, 4]), ALU.mult)
                S.dma("sp", btmp[0:1, :], I["b_sg"][:, :], "p_bs0")
                S.dma("sp", btmp[32:33, :], I["b_sg"][:, :], "p_bs1")
                bc2 = bcomb[:].rearrange("p g t -> p (g t)")
                S.copy("dve", bc2[0:1, :], btmp[0:1, :])
                S.copy("dve", bhi[32:33, :], btmp[32:33, :])
                S.tt(bc2[32:33, :], btmp[32:33, :], bhi[32:33, :], ALU.subtract)
                S.copy("dve", bcomb_s[:].rearrange("p g (b t) -> p g b t", t=4),
                       bcomb[:, :, 0:4].unsqueeze(2).broadcast_to([128, 8, NB, 4]))
            S.transposes([(PS[:, 6, j * 32:(j + 1) * 32], sct[:, j * 128:(j + 1) * 128]) for j in range(8)], ident)
            S.copy("dve", C["scT"][:], PS[:, 6, 0:256].rearrange("p (j r) -> p j r", r=32))
            gvB = V(R2, 65536, F32, [CD])
            S.dma("sp", gvB, bass.AP(I["g_v"].tensor, 0, [[0, 128], [1, CD]]), "p_gv")
            pend.append(sg_setup)

            if STAGE < 1:
                ada_all()
                if DEBUG:
                    dbgt = V(MISC, 0, F32, [2176])
                    S.dma("sp", O["dbg"][:, 0:96 * NR], modT[:].rearrange("p a b -> p (a b)"), "o_dbg")
                    S.copy("dve", dbgt[:, 0:1090], hT[:, 3, :])
                    S.dma("sp", O["dbg"][:, 4096:4096 + 1090], dbgt[:, 0:1090], "o_dbg3")
                return

            def mm_group(bset, pan, i, rhs_fn, nk, ncols=(363, 363, 362), tag="", split_k=False):
                if split_k:
                    for k in range(nk):
                        S.mms([(PS[:, bset[t], 0:ncols[t]], pan[:, k, i * 128:(i + 1) * 128], rhs_fn(k, t, ncols[t]),
                                k == 0, k == nk - 1) for t in range(2)], tag=tag)
                    S.mms([(PS[:, bset[2], 0:ncols[2]], pan[:, k, i * 128:(i + 1) * 128], rhs_fn(k, 2, ncols[2]),
                            k == 0, k == nk - 1) for k in range(nk)], tag=tag)
                    flush()
                    return
                lst = []
                for k in range(nk):
                    for t in range(3):
                        lst.append((PS[:, bset[t], 0:ncols[t]], pan[:, k, i * 128:(i + 1) * 128], rhs_fn(k, t, ncols[t]),
                                    k == 0, k == nk - 1))
                S.mms(lst, tag=tag)
                flush()

            def rhs_h(buf):
                return lambda k, t, n: buf[:, k, TT[t][0]:TT[t][0] + n]

            ada_div = [0]

            def ada_tick():
                ada_div[0] += 1
                if ada_div[0] % 3 == 0 and ada_state[0] < 22:
                    ada_round()

            ya = V(R1, 34880, BF16, [8, T])
            yb = V(R1, 34880 + 17408, BF16, [8, T])
            zbuf = V(R2, 0, F32, [4, 1122])
            convb = V(R2, 17952, F32, [4, T])
            zst = V(R2, 35360, F32, [4, 34])
            cso = V(R2, 35904, F32, [CD])[0:34]
            wcv = C["wconvT"]

            def zs_view(jj):
                return zbuf[:, jj, 1026:1122].rearrange("p (b s) -> p b s", s=6)

            for hf in range(2):
                S.copy("act", zbuf[:, :, 1026:1122].rearrange("p j (b s) -> p j b s", s=6)[:, :, :, 0:2],
                       C["scT"][:, 4 * hf:4 * hf + 4, :].rearrange("p j (b k) -> p j b k", k=2))
                for which in range(3):
                    colbase = [1024, 2048, 0][which] + hf * 512
                    for pi in range(2):
                        pan = PSt.acquire(I["w_in"], 0, KC, colbase + pi * 256, 256)
                        for i in range(2):
                            jj = 2 * pi + i
                            j = 4 * hf + jj
                            bset = next_set()
                            n2 = 364 if which < 2 else 362
                            mm_group(bset, pan, i, rhs_h(hT), KC, (363, 363, n2), tag="A%d_%d" % (which, j),
                                     split_k=(hf == 0 and which == 0 and pi == 0 and i == 0))
                            p0, p1, p2 = (PS[:, bset[0], :], PS[:, bset[1], :], PS[:, bset[2], :])
                            if which == 0:
                                S.copy("act", zbuf[:, jj, 2:365], p0[:, 0:363])
                                S.copy("act", zbuf[:, jj, 365:728], p1[:, 0:363])
                                S.copy("act", zbuf[:, jj, 728:1026], p2[:, 0:298])
                                S.copy("act", zs_view(jj)[:, :, 2:6], p2[:, 298:362].rearrange("p (b s) -> p b s", s=4))
                                S.copy("act", zbuf[:, jj, 0:2], p2[:, 362:364])
                            elif which == 1:
                                S.tt(zbuf[:, jj, 2:365], p0[:, 0:363], zbuf[:, jj, 2:365], ALU.mult)
                                S.tt(zbuf[:, jj, 365:728], p1[:, 0:363], zbuf[:, jj, 365:728], ALU.mult)
                                S.tt(zbuf[:, jj, 728:1026], p2[:, 0:298], zbuf[:, jj, 728:1026], ALU.mult)
                                S.tt(zs_view(jj)[:, :, 2:6], p2[:, 298:362].rearrange("p (b s) -> p b s", s=4),
                                     zs_view(jj)[:, :, 2:6], ALU.mult)
                                S.stt(zbuf[:, jj, 0:2], p2[:, 362:364], C["hmaskB"][:, 0:1], zbuf[:, jj, 0:2], ALU.mult, ALU.mult)
                                cb = convb[:, jj, 0:TP]
                                S.ts(cb, zbuf[:, jj, 0:TP], wcv[:, j, 0:1], None, ALU.mult)
                                S.stt(cb, zbuf[:, jj, 1:TP + 1], wcv[:, j, 1:2], cb, ALU.mult, ALU.add)
                                S.stt(cb, zbuf[:, jj, 2:TP + 2], wcv[:, j, 2:3], cb, ALU.mult, ALU.add)
                                cbs = convb[:, jj, TP:T].rearrange("p (b s) -> p b s", s=4)
                                zs = zs_view(jj)
                                S.ts(cbs, zs[:, :, 0:4], wcv[:, j, 0:1], None, ALU.mult)
                                S.stt(cbs, zs[:, :, 1:5], wcv[:, j, 1:2], cbs, ALU.mult, ALU.add)
                                S.stt(cbs, zs[:, :, 2:6], wcv[:, j, 2:3], cbs, ALU.mult, ALU.add)
                                S.copy("act", zst[:, jj, 0:2], zbuf[:, jj, 1024:1026])
                                S.copy("act", zst[:, jj, 2:34].rearrange("p (b k) -> p b k", k=2), zs[:, :, 4:6])
                                def cs_follow(jj=jj, j=j):
                                    S.transposes([(PS[0:34, 7, 0:128], zst[:, jj, :])], ident)
                                    S.copy("act", cso[:, j * 128:(j + 1) * 128], PS[0:34, 7, 0:128])
                                pend.append(cs_follow)
                            else:
                                for t in range(3):
                                    o0, n = TT[t]
                                    S.tt(ya[:, j, o0:o0 + n], PS[:, bset[t], 0:n], convb[:, jj, o0:o0 + n], ALU.mult)
                        PSt.release()
                        ada_tick()
            flush()
            S.dma("sp", O["cvo"][:, :], cso, "o_cv")

            gv = V(R2, 0, F32, [9, CD])
            vn = V(R2, 36864, BF16, [9, CD])
            vn32 = V(R2, 55296, F32, [2, CD])
            vjunk = V(R2, 63488, BF16, [CD])
            for pv in range(4):
                pan = PSt.acquire(I["w_in"], 0, KC, 4096 + pv * 256, 256)
                for c in range(9):
                    rows = 128 if c < 8 else TS
                    idx = pv * 9 + c
                    b = idx % 6
                    o = PS[0:rows, b, 0:256]
                    S.mms([(o, hT[:, k, c * 128:c * 128 + rows], pan[:, k, :], k == 0, k == KC - 1) for k in range(KC)],
                          tag="v%d_%d" % (pv, c))
                    S.act(gv[0:rows, c, pv * 256:(pv + 1) * 256], o, AF.Gelu_apprx_tanh)
                    if pv == 3:
                        S.act(vjunk[0:rows], gv[0:rows, c, :], AF.Square, accum_out=C["vss"][0:rows, c:c + 1])
                PSt.release()
                ada_tick()
            vss, vsd, vrs = C["vss"], C["vsd"], C["vrs"]
            S.act(vsd[:, 0:9], vss[:, 0:9], AF.Sqrt, bias=EPS, scale=1.0 / CD)
            S.recip(vrs[:, 0:9], vsd[:, 0:9])
            for c in range(9):
                rows = 128 if c < 8 else TS
                S.stt(vn[0:rows, c, :], gv[0:rows, c, :], vrs[0:rows, c:c + 1], gvB[0:rows], ALU.mult, ALU.mult)
            for c in (7, 8):
                rows = 128 if c < 8 else TS
                S.stt(vn32[0:rows, c - 7, :], gv[0:rows, c, :], vrs[0:rows, c:c + 1], gvB[0:rows], ALU.mult, ALU.mult)
            S.dma("sp", O["sgp"][:, :], vn32[:, 0, :], "o_sgp")
            S.dma("sp", O["sgs"][:, :], vn32[0:TS, 1, :], "o_sgs")

            gu = V(R2, 0, F32, [2, T])
            for pu in range(4):
                pan = PSt.acquire(I["w_in"], 0, KC, 3072 + pu * 256, 256)
                for i in range(2):
                    g = 2 * pu + i
                    bset = next_set()
                    mm_group(bset, pan, i, rhs_h(hT), KC, tag="u%d" % g)
                    for t in range(3):
                        o0, n = TT[t]
                        S.act(gu[:, g % 2, o0:o0 + n], PS[:, bset[t], 0:n], AF.Gelu_apprx_tanh)
                for i in range(2):
                    g = 2 * pu + i
                    sset = next_set()
                    lst = []
                    for c in range(8):
                        o = PS[:, sset[c // 4], (c % 4) * 128:(c % 4 + 1) * 128]
                        lst.append((o, vn[:, c, g * 128:(g + 1) * 128], C["WT"][:, g, :], True, False))
                        lst.append((o, bsel[:], bcomb[:, g, :], False, True))
                    o = PS[:, sset[2], 0:TS]
                    lst.append((o, vn[0:TS, 8, g * 128:(g + 1) * 128], C["Wblk"][:, g, :], True, False))
                    lst.append((o, bsel[:], bcomb_s[:, g, :], False, True))
                    S.mms(lst, tag="sp%d" % g)
                    for t in range(3):
                        o0, n = TTSP[t]
                        S.tt(yb[:, g, o0:o0 + n], PS[:, sset[t], 0:n], gu[:, g % 2, o0:o0 + n], ALU.mult)
                PSt.release()
                ada_tick()
            if STAGE < 2:
                ada_all()
                return

            merged = V(R2, 0, BF16, [KC, T])
            sab = V(R2, 34816, F32, [8, T])

            def rhs_y(buf):
                return lambda k, t, n: buf[:, k, TT[t][0]:TT[t][0] + n]

            for pp in range(8):
                par = pp % 2
                for which in range(4):
                    if which == 0:
                        pan = PSt.acquire(I["w_in"], 0, KC, 5120 + pp * 256, 256)
                    elif which == 2:
                        pan = PSt.acquire(I["w_in"], 0, KC, 7168 + pp * 256, 256)
                    elif which == 1:
                        pan = PSt.acquire(I["w_pa"], 0, 8, pp * 256, 256)
                    else:
                        pan = PSt.acquire(I["w_pb"], 0, 8, pp * 256, 256)
                    for i in range(2):
                        m = 2 * pp + i
                        bset = next_set()
                        sa = sab[:, par * 4 + i, :]
                        sbb = sab[:, par * 4 + 2 + i, :]
                        if which == 0 or which == 2:
                            mm_group(bset, pan, i, rhs_h(hT), KC, tag="g%d_%d" % (which, m))
                            dst = sa if which == 0 else sbb
                            for t in range(3):
                                o0, n = TT[t]
                                S.act(dst[:, o0:o0 + n], PS[:, bset[t], 0:n], AF.Sigmoid)
                        elif which == 1:
                            mm_group(bset, pan, i, rhs_y(ya), 8, tag="pa%d" % m)
                            for t in range(3):
                                o0, n = TT[t]
                                S.tt(sa[:, o0:o0 + n], PS[:, bset[t], 0:n], sa[:, o0:o0 + n], ALU.mult)
                        else:
                            mm_group(bset, pan, i, rhs_y(yb), 8, tag="pb%d" % m)
                            for t in range(3):
                                o0, n = TT[t]
                                S.tt(sbb[:, o0:o0 + n], PS[:, bset[t], 0:n], sbb[:, o0:o0 + n], ALU.mult)
                            S.tt(merged[:, m, :], sbb, sa, ALU.add)
                    PSt.release()
                    ada_tick()
            while ada_state[0] < 22:
                ada_round()
                flush()
            if STAGE < 3:
                ada_all()
                return

            x1 = V(R1, 0, F32, [KC, T])
            xsm = [V(R2, 34816 + i * 4608, F32, [9, 128]) for i in range(2)]
            xTsb = [V(R2, 44032 + i * 4096, F32, [TP]) for i in range(2)]
            tmp64 = C["tmp64"]
            sqf = V(R2, 52480, F32, [T])
            acc = V(R2, 56832, F32, [T])
            accb = V(R2, 61184, BF16, [T])
            gt1 = modT[:, 32:48, :]
            for m in range(KC):
                if m % 2 == 0:
                    pan = PSt.acquire(I["w_out"], 0, KC, (m // 2) * 256, 256)
                i = m % 2
                xs = xsm[m % 2]
                S.dma("sp", xs[:, 0:8, :], I["xp"][:, m * 128:(m + 1) * 128].rearrange("(c p) n -> p c n", p=128), "xsA%d" % (m % 2))
                S.dma("sp", xs[0:TS, 8, :], I["xsh"][0:TS, m * 128:(m + 1) * 128], "xsB%d" % (m % 2))
                bset = next_set()
                lst = [(PS[:, 6 + c // 4, (c % 4) * 128:(c % 4 + 1) * 128], xs[:, c, :]) for c in range(8)]
                lst.append((PS[:, bset[2], 384:384 + TS], xs[0:TS, 8, :]))
                S.transposes(lst, ident, tag="xT%d" % m)
                xT = xTsb[m % 2]
                S.copy("act", xT[:, 0:512], PS[:, 6, :])
                S.copy("act", xT[:, 512:1024], PS[:, 7, :])
                mm_group(bset, pan, i, rhs_y(merged), KC, tag="out%d" % m, split_k=(m == 0))
                S.stt(x1[:, m, 0:363], PS[:, bset[0], 0:363], gt1[:, m, 0:1], xT[:, 0:363], ALU.mult, ALU.add)
                S.stt(x1[:, m, 363:726], PS[:, bset[1], 0:363], gt1[:, m, 0:1], xT[:, 363:726], ALU.mult, ALU.add)
                S.stt(x1[:, m, 726:TP], PS[:, bset[2], 0:298], gt1[:, m, 0:1], xT[:, 726:TP], ALU.mult, ALU.add)
                S.tt(tmp64[:].rearrange("p (b s) -> p b s", s=4), PS[:, bset[2], 298:362].rearrange("p (b s) -> p b s", s=4),
                     gt1[:, m, 1:17].unsqueeze(2).broadcast_to([128, NB, 4]), ALU.mult)
                S.tt(x1[:, m, TP:T], tmp64[:], PS[:, bset[2], 384:384 + TS], ALU.add)
                if m == 0:
                    S.act(acc, x1[:, m, :], AF.Square)
                else:
                    S.act(sqf, x1[:, m, :], AF.Square)
                    S.tt(acc if m < KC - 1 else accb, acc, sqf, ALU.add)
                if m % 2 == 1:
                    PSt.release()
            if STAGE < 4:
                return

            h2T = V(R2, 0, BF16, [KC, T])
            tmpx = [V(R2, 34816 + i * 4352, F32, [T]) for i in range(3)]
            a2 = C["a2"]

            def rms_bcast(src_bf, bset_sum, bset_out):
                S.mms([(PS[:, bset_sum[t], 0:TT[t][1]], onesbf[:], src_bf[:, TT[t][0]:TT[t][0] + TT[t][1]], True, True)
                       for t in range(3)], tag="rmsum")
                for t in range(3):
                    n = TT[t][1]
                    S.act(PS[:, bset_out[t], 0:n], PS[:, bset_sum[t], 0:n], AF.Ln, bias=EPS, scale=1.0 / D)
                for t in range(3):
                    n = TT[t][1]
                    S.act(PS[:, bset_out[t], 0:n], PS[:, bset_out[t], 0:n], AF.Exp, scale=-0.5)

            s_sum, s_out = next_set(), next_set()
            rms_bcast(accb, s_sum, s_out)
            ada_all()
            txs = V(R2, 34816 + 3 * 4352, F32, [KC, TS])

            def h2_M(k):
                tx = tmpx[k % 3]
                S.tt(tx[:, 0:363], PS[:, s_out[0], 0:363], x1[:, k, 0:363], ALU.mult)
                S.tt(tx[:, 363:726], PS[:, s_out[1], 0:363], x1[:, k, 363:726], ALU.mult)
                S.tt(tx[:, 726:TP], PS[:, s_out[2], 0:298], x1[:, k, 726:TP], ALU.mult)
                S.tt(txs[:, k, :], PS[:, s_out[2], 298:362], x1[:, k, TP:T], ALU.mult)
                S.act(h2T[:, k, 0:TP], tx[:, 0:TP], AF.Identity, bias=modT[:, 48 + k, 0:1], scale=a2[:, k, 0:1])

            for k in range(KC):
                h2_M(k)
            txs4 = txs.rearrange("p k (b s) -> p k b s", s=4)
            S.tt(txs4, txs4, a2[:, :, 1:17].unsqueeze(3).broadcast_to([128, KC, NB, 4]), ALU.mult)
            S.tt(h2T[:, :, TP:T].rearrange("p k (b s) -> p k b s", s=4), txs4,
                 modT[:, 48:64, 1:17].unsqueeze(3).broadcast_to([128, KC, NB, 4]), ALU.add)

            if STAGE < 5:
                return
            actb = V(R2, 34816, BF16, [16, T])
            sg = V(MISC, 0, F32, [2, T])
            gt2 = modT[:, 80:96, :]
            for (j0, j1) in [(0, 16), (16, 32), (32, 44)]:
                for j in range(j0, j1, 2):
                    pg = PSt.acquire(I["w_ffn_in"], 0, KC, j * 128, 256)
                    pu_ = PSt.acquire(I["w_ffn_in"], 0, KC, DFF + j * 128, 256)
                    for i in range(2):
                        bset = next_set()
                        mm_group(bset, pg, i, rhs_y(h2T), KC, tag="gate%d" % (j + i), split_k=(j == 0 and i == 0))
                        for t in range(3):
                            o0, n = TT[t]
                            S.act(sg[:, i, o0:o0 + n], PS[:, bset[t], 0:n], AF.Silu)
                        bset = next_set()
                        mm_group(bset, pu_, i, rhs_y(h2T), KC, tag="up%d" % (j + i))
                        for t in range(3):
                            o0, n = TT[t]
                            S.tt(actb[:, j - j0 + i, o0:o0 + n], PS[:, bset[t], 0:n], sg[:, i, o0:o0 + n], ALU.mult)
                    PSt.release()
                    PSt.release()
                nk = j1 - j0
                for mp in range(8):
                    pan = PSt.acquire(I["w_ffn_out"], j0 * 128, nk, mp * 256, 256)
                    for i in range(2):
                        m = 2 * mp + i
                        bset = next_set()
                        mm_group(bset, pan, i, rhs_y(actb), nk, tag="fo%d_%d" % (j0, m))
                        S.stt(x1[:, m, 0:363], PS[:, bset[0], 0:363], gt2[:, m, 0:1], x1[:, m, 0:363], ALU.mult, ALU.add)
                        S.stt(x1[:, m, 363:726], PS[:, bset[1], 0:363], gt2[:, m, 0:1], x1[:, m, 363:726], ALU.mult, ALU.add)
                        S.stt(x1[:, m, 726:TP], PS[:, bset[2], 0:298], gt2[:, m, 0:1], x1[:, m, 726:TP], ALU.mult, ALU.add)
                        t64 = tmp64[:].rearrange("p (b s) -> p b s", s=4)
                        S.tt(t64, PS[:, bset[2], 298:362].rearrange("p (b s) -> p b s", s=4),
                             gt2[:, m, 1:17].unsqueeze(2).broadcast_to([128, NB, 4]), ALU.mult)
                        S.tt(x1[:, m, TP:T], x1[:, m, TP:T], tmp64[:], ALU.add)
                    PSt.release()

            if STAGE < 6:
                return
            yst = [V(R2, i * 8192, F32, [D]) for i in range(2)]
            gfinB = V(R2, 16384, F32, [D])
            S.dma("sp", gfinB, bass.AP(I["g_final"].tensor, 0, [[0, 128], [1, D]]), "p_gf")
            fss, fsd, frs = C["fss"], C["fsd"], C["frs"]
            fjunk = V(R2, 24576, BF16, [D])
            s_old = 3 * (set_ctr[0] % 2)
            s_new = 3 * ((set_ctr[0] + 1) % 2)
            fbanks = [[s_old, s_old + 1, s_old + 2, 6], [s_new, s_new + 1, s_new + 2, 7]]
            for c in range(9):
                rows = 128 if c < 8 else TS
                bk = fbanks[c % 2]
                for q in range(4):
                    S.transposes([(PS[0:rows, bk[q], mm * 128:(mm + 1) * 128], x1[:, 4 * q + mm, c * 128:c * 128 + rows])
                                  for mm in range(4)], ident, tag="yT%d_%d" % (c, q))
                fq = C["fq"][:, 4 * (c % 2):4 * (c % 2) + 4]
                for q in range(4):
                    S.act(fjunk[0:rows, q * 512:(q + 1) * 512], PS[0:rows, bk[q], :], AF.Square, accum_out=fq[0:rows, q:q + 1])
                S.add("dve", lambda e, c=c, rows=rows, fq=fq: e.tensor_reduce(out=fss[0:rows, c:c + 1], in_=fq[0:rows, 0:4],
                                                                              axis=mybir.AxisListType.X, op=ALU.add),
                      [fq[0:rows, 0:4]], [fss[0:rows, c:c + 1]])
                S.act(fsd[0:rows, c:c + 1], fss[0:rows, c:c + 1], AF.Sqrt, bias=EPS, scale=1.0 / D)
                S.recip(frs[0:rows, c:c + 1], fsd[0:rows, c:c + 1])
                y = yst[c % 2][0:rows]
                for q in range(4):
                    S.stt(y[:, q * 512:(q + 1) * 512], PS[0:rows, bk[q], :], frs[0:rows, c:c + 1],
                          gfinB[0:rows, q * 512:(q + 1) * 512], ALU.mult, ALU.mult)
                if c < 8:
                    S.dma("sp", O["yp"][c * 128:(c + 1) * 128, :], y, "o_y%d" % (c % 2))
                else:
                    S.dma("sp", O["ys"][:, :], y, "o_y%d" % (c % 2))

        S0 = Sched()
        P0 = PanelStream(S0, ring, plan=None)
        construct(S0, P0)
        S1 = Sched()
        P1 = PanelStream(S1, ring, plan=P0.rec)
        construct(S1, P1)
        assert P1.next_use == len(P0.rec)
        S1.emit(nc)
    return nc


_NC_CACHE = {}


def kernel(x_prompt, x_sample, state_conv, c_prompt, c_sample, g_mix, g_ffn, w_ada, b_ada, w_in, w_conv, g_v,
           w_sg, b_sg, w_pa, w_pb, w_out, w_ffn_in, w_ffn_out, g_final):
    f = lambda a: np.ascontiguousarray(np.asarray(a, dtype=np.float32))
    x_prompt, x_sample, state_conv = f(x_prompt), f(x_sample), f(state_conv)
    c_prompt, c_sample = f(c_prompt), f(c_sample)
    shared = dict(
        w_ada=f(w_ada)[0], b_ada=f(b_ada)[0].reshape(96, 128), w_in=f(w_in)[0], w_conv=f(w_conv)[0],
        g_v=f(g_v)[0].reshape(1, CD), w_sg=f(w_sg)[0], b_sg=f(b_sg)[0].reshape(1, 1024),
        w_pa=f(w_pa)[0], w_pb=f(w_pb)[0], w_out=f(w_out)[0], w_ffn_in=f(w_ffn_in)[0], w_ffn_out=f(w_ffn_out)[0],
        g_mix=f(g_mix)[0].reshape(1, D), g_ffn=f(g_ffn)[0].reshape(1, D), g_final=f(g_final).reshape(1, D),
    )
    in_maps = []
    for i in range(NCORES):
        b, half = i // 2, i % 2
        t0 = half * TP
        xs = x_sample[NB * i:NB * (i + 1)].reshape(TS, D)
        if half == 1:
            halo = x_prompt[b, t0 - 2:t0]
            hm = np.ones((1, 1), np.float32)
        else:
            halo = np.zeros((2, D), np.float32)
            hm = np.zeros((1, 1), np.float32)
        m = dict(shared)
        m.update(
            xp=np.ascontiguousarray(x_prompt[b, t0:t0 + TP]),
            xsh=np.ascontiguousarray(np.concatenate([xs, halo], axis=0)),
            hmask=hm,
            sc=np.ascontiguousarray(state_conv[0, NB * i:NB * (i + 1)].reshape(2 * NB, CD)),
            cvec=np.ascontiguousarray(np.concatenate([c_prompt[b:b + 1], c_sample[NB * i:NB * (i + 1)]], axis=0)),
        )
        in_maps.append(m)
    if "nc" not in _NC_CACHE:
        _NC_CACHE["nc"] = build_nc()
    res = run_bass_kernel_spmd(_NC_CACHE["nc"], in_maps, core_ids=list(range(NCORES)))
    R = res.results
    B = x_prompt.shape[0]
    y_prompt = np.zeros((B, 2 * TP, D), np.float32)
    y_sample = np.zeros((NB * NCORES, 4, D), np.float32)
    conv_p = np.zeros((1, B, 2, CD), np.float32)
    conv_s = np.zeros((1, NB * NCORES, 2, CD), np.float32)
    sgv_p = np.zeros((1, B, 128, 8, 128), np.float32)
    sgv_s = np.zeros((1, NB * NCORES, 4, 8, 128), np.float32)
    for i in range(NCORES):
        b, half = i // 2, i % 2
        r = R[i]
        y_prompt[b, half * TP:(half + 1) * TP] = r["yp"]
        y_sample[NB * i:NB * (i + 1)] = r["ys"].reshape(NB, 4, D)
        conv_s[0, NB * i:NB * (i + 1)] = r["cvo"][2:].reshape(NB, 2, CD)
        sgv_s[0, NB * i:NB * (i + 1)] = r["sgs"].reshape(NB, 4, 8, 128)
        if half == 1:
            conv_p[0, b] = r["cvo"][0:2]
            sgv_p[0, b] = r["sgp"].reshape(128, 8, 128)
    _NC_CACHE["last"] = R
    return (y_prompt, y_sample, conv_p, conv_s, sgv_p, sgv_s)
```
